# Optimizing a Trainium2 kernel written in Bass

```python
import math
import jax, jax.numpy as jnp
from jax import lax
import numpy as np

D_MODEL = 1024
BATCH = 16
SEQ = 2048
DEPTH = 2

N_MIXERS = 2
N_ATTN_LAYERS = (DEPTH + 1) // 2
N_SSM_LAYERS = DEPTH // 2
HEAD_DIM = 64
ATTN_HEADS = D_MODEL // HEAD_DIM
ATTN_GROUPS = ((128, 1), (512, 4), (2048, 16))
N_ATTN_GROUPS = len(ATTN_GROUPS)
ATTN_WIDTH = ATTN_HEADS * HEAD_DIM
QKV_WIDTH = N_ATTN_GROUPS * 3 * ATTN_WIDTH
ALIBI_MAX_BIAS = 8.0
NEG_INF = -1e30
SSM_WIDTH = D_MODEL
SSM_GROUP_CH = 16
SSM_GROUPS = SSM_WIDTH // SSM_GROUP_CH
SSM_STATE = 64
SSM_DIRS = 2
SSM_DT_MIN = 1e-3
SSM_DT_MAX = 1e-1
FFN_HIDDEN = -(-8 * D_MODEL // (3 * 256)) * 256
NORM_EPS = 1e-6

kernel_name = "hybrid_dilated_attn_s5_encoder"


def rms_norm(x, gain):
    xf = x.astype(jnp.float32)
    y = xf * lax.rsqrt(jnp.mean(xf * xf, axis=-1, keepdims=True) + NORM_EPS)
    return (y * gain.astype(jnp.float32)).astype(x.dtype)


def alibi_slopes(n_heads):
    return 2.0 ** (-ALIBI_MAX_BIAS * jnp.arange(1, n_heads + 1, dtype=jnp.float32) / n_heads)


def dilated_band_attention(q, k, v, window, dilation, slopes):
    b, s, h, e = q.shape
    half = window // 2 // dilation
    length = s // dilation
    n_blk = -(-length // half)
    lp = n_blk * half

    def to_residue(t):
        return t.reshape(b, length, dilation, h, e).transpose(0, 2, 1, 3, 4)

    qr, kr, vr = to_residue(q), to_residue(k), to_residue(v)
    qb = jnp.pad(qr, ((0, 0), (0, 0), (0, lp - length), (0, 0), (0, 0)))
    qb = qb.reshape(b, dilation, n_blk, half, h, e)

    def band(t):
        tp = jnp.pad(t, ((0, 0), (0, 0), (half, lp - length + half), (0, 0), (0, 0)))
        tp = tp.reshape(b, dilation, n_blk + 2, half, h, e)
        return jnp.concatenate([tp[:, :, :-2], tp[:, :, 1:-1], tp[:, :, 2:]], axis=3)

    kw, vw = band(kr), band(vr)
    rel = jnp.arange(3 * half)[None, :] - half - jnp.arange(half)[:, None]
    kpos = jnp.arange(n_blk)[:, None] * half + jnp.arange(3 * half)[None, :] - half
    valid = (jnp.abs(rel)[None] <= half) & ((kpos >= 0) & (kpos < length))[:, None, :]
    bias = -slopes[:, None, None] * (jnp.abs(rel) * dilation).astype(jnp.float32)

    scores = jnp.einsum('brnqhe,brnkhe->brnhqk', qb, kw, preferred_element_type=jnp.float32)
    scores = jnp.where(valid[None, None, :, None], scores + bias[None, None, None], NEG_INF)
    m = jnp.max(scores, axis=-1, keepdims=True)
    p = jnp.exp(scores - m)
    den = jnp.sum(p, axis=-1, keepdims=True)
    o = jnp.einsum('brnhqk,brnkhe->brnqhe', p, vw.astype(jnp.float32))
    o = o / den.transpose(0, 1, 2, 4, 3, 5)
    lse = (m + jnp.log(den))[..., 0].transpose(0, 1, 2, 4, 3)
    o = o.reshape(b, dilation, lp, h, e)[:, :, :length].transpose(0, 2, 1, 3, 4).reshape(b, s, h, e)
    lse = lse.reshape(b, dilation, lp, h)[:, :, :length].transpose(0, 2, 1, 3).reshape(b, s, h)
    return o, lse


def dilated_attention_mixer(hn, w_qkv, w_out):
    b, s, _ = hn.shape
    qkv = (hn @ w_qkv).reshape(b, s, N_ATTN_GROUPS, 3, ATTN_HEADS, HEAD_DIM)
    slopes = alibi_slopes(ATTN_HEADS)
    outs, lses = [], []
    for g, (window, dilation) in enumerate(ATTN_GROUPS):
        q = qkv[:, :, g, 0] * (HEAD_DIM ** -0.5)
        o, lse = dilated_band_attention(q, qkv[:, :, g, 1], qkv[:, :, g, 2], window, dilation, slopes)
        outs.append(o)
        lses.append(lse)
    weights = jax.nn.softmax(jnp.stack(lses, axis=0), axis=0)
    o = jnp.sum(weights[..., None] * jnp.stack(outs, axis=0), axis=0)
    return o.reshape(b, s, ATTN_WIDTH).astype(hn.dtype) @ w_out


def complex_affine_combine(first, second):
    a1r, a1i, b1r, b1i = first
    a2r, a2i, b2r, b2i = second
    return (a1r * a2r - a1i * a2i,
            a1r * a2i + a1i * a2r,
            a2r * b1r - a2i * b1i + b2r,
            a2r * b1i + a2i * b1r + b2i)


def s5_direction(u, a_re, a_im, log_dt, b_re, b_im, c_re, c_im, reverse):
    a_re = a_re.astype(jnp.float32)
    a_im = a_im.astype(jnp.float32)
    dt = jnp.exp(log_dt.astype(jnp.float32))[:, None]
    mag = jnp.exp(a_re * dt)
    lr, li = mag * jnp.cos(a_im * dt), mag * jnp.sin(a_im * dt)
    inv = 1.0 / (a_re * a_re + a_im * a_im)
    fr = ((lr - 1.0) * a_re + li * a_im) * inv
    fi = (li * a_re - (lr - 1.0) * a_im) * inv
    b_re = b_re.astype(jnp.float32)
    b_im = b_im.astype(jnp.float32)
    bbr = fr[..., None] * b_re - fi[..., None] * b_im
    bbi = fr[..., None] * b_im + fi[..., None] * b_re
    bur = jnp.einsum('bsgc,gpc->bsgp', u, bbr)
    bui = jnp.einsum('bsgc,gpc->bsgp', u, bbi)
    s = u.shape[1]
    ar = jnp.broadcast_to(lr[None, None], (1, s) + lr.shape)
    ai = jnp.broadcast_to(li[None, None], (1, s) + li.shape)
    _, _, xr, xi = lax.associative_scan(complex_affine_combine, (ar, ai, bur, bui),
                                        reverse=reverse, axis=1)
    return (jnp.einsum('bsgp,gcp->bsgc', xr, c_re.astype(jnp.float32))
            - jnp.einsum('bsgp,gcp->bsgc', xi, c_im.astype(jnp.float32)))


def s5_mixer(hn, w_in, a_re, a_im, log_dt, b_re, b_im, c_re, c_im, d_skip, w_glu):
    b, s, _ = hn.shape
    u = (hn @ w_in).astype(jnp.float32).reshape(b, s, SSM_GROUPS, SSM_GROUP_CH)
    y = d_skip.astype(jnp.float32).reshape(SSM_GROUPS, SSM_GROUP_CH) * u
    for direction in range(SSM_DIRS):
        y = y + s5_direction(u, a_re[direction], a_im[direction], log_dt[direction],
                             b_re[direction], b_im[direction], c_re[direction], c_im[direction],
                             reverse=(direction == 1))
    y = jax.nn.gelu(y.reshape(b, s, SSM_WIDTH)).astype(hn.dtype)
    ag = y @ w_glu
    return ag[..., :D_MODEL] * jax.nn.sigmoid(ag[..., D_MODEL:])


def swiglu_ffn(hn, w_in, w_out):
    gu = hn @ w_in
    return (jax.nn.silu(gu[..., :FFN_HIDDEN]) * gu[..., FFN_HIDDEN:]) @ w_out


def setup_inputs(seed: int = 0) -> dict:
    key = jax.random.key(seed)
    ks = jax.random.split(key, 22)
    f32 = jnp.float32
    na, ns = N_ATTN_LAYERS, N_SSM_LAYERS
    g, p, c = SSM_GROUPS, SSM_STATE, SSM_GROUP_CH
    resid = (2 * DEPTH) ** -0.5

    def normal(k, shape, scale):
        return jax.random.normal(k, shape, f32) * scale

    def gain(k, shape):
        return 1.0 + 0.05 * jax.random.normal(k, shape, f32)

    return {
        "x": normal(ks[0], (BATCH, SEQ, D_MODEL), 1.0),
        "attn_norm": gain(ks[1], (na, D_MODEL)),
        "w_qkv": normal(ks[2], (na, D_MODEL, QKV_WIDTH), D_MODEL ** -0.5),
        "w_attn_out": normal(ks[3], (na, ATTN_WIDTH, D_MODEL), ATTN_WIDTH ** -0.5 * resid),
        "ssm_norm": gain(ks[4], (ns, D_MODEL)),
        "w_ssm_in": normal(ks[5], (ns, D_MODEL, SSM_WIDTH), D_MODEL ** -0.5),
        "a_re": -0.5 + 0.05 * jax.random.uniform(ks[6], (ns, SSM_DIRS, g, p), f32, -1.0, 1.0),
        "a_im": math.pi * jnp.arange(p, dtype=f32) + 0.05 * normal(ks[7], (ns, SSM_DIRS, g, p), 1.0),
        "log_dt": jax.random.uniform(ks[8], (ns, SSM_DIRS, g), f32,
                                     math.log(SSM_DT_MIN), math.log(SSM_DT_MAX)),
        "b_re": normal(ks[9], (ns, SSM_DIRS, g, p, c), (2 * c) ** -0.5),
        "b_im": normal(ks[10], (ns, SSM_DIRS, g, p, c), (2 * c) ** -0.5),
        "c_re": normal(ks[11], (ns, SSM_DIRS, g, c, p), p ** -0.5),
        "c_im": normal(ks[12], (ns, SSM_DIRS, g, c, p), p ** -0.5),
        "d_skip": normal(ks[13], (ns, SSM_WIDTH), 1.0),
        "w_glu": normal(ks[14], (ns, SSM_WIDTH, 2 * D_MODEL), SSM_WIDTH ** -0.5 * resid),
        "ffn_norm": gain(ks[15], (DEPTH, D_MODEL)),
        "w_ffn_in": normal(ks[16], (DEPTH, D_MODEL, 2 * FFN_HIDDEN), D_MODEL ** -0.5),
        "w_ffn_out": normal(ks[17], (DEPTH, FFN_HIDDEN, D_MODEL), FFN_HIDDEN ** -0.5 * resid),
        "final_norm": gain(ks[18], (D_MODEL,)),
    }


def reference(x, attn_norm, w_qkv, w_attn_out, ssm_norm, w_ssm_in, a_re, a_im, log_dt,
              b_re, b_im, c_re, c_im, d_skip, w_glu, ffn_norm, w_ffn_in, w_ffn_out, final_norm):
    h = x
    for layer in range(DEPTH):
        slot = layer // N_MIXERS
        if layer % N_MIXERS == 0:
            h = h + dilated_attention_mixer(rms_norm(h, attn_norm[slot]), w_qkv[slot], w_attn_out[slot])
        else:
            h = h + s5_mixer(rms_norm(h, ssm_norm[slot]), w_ssm_in[slot], a_re[slot], a_im[slot],
                             log_dt[slot], b_re[slot], b_im[slot], c_re[slot], c_im[slot],
                             d_skip[slot], w_glu[slot])
        h = h + swiglu_ffn(rms_norm(h, ffn_norm[layer]), w_ffn_in[layer], w_ffn_out[layer])
    return rms_norm(h, final_norm)
```

```python
import contextlib
import math
import numpy as np
import concourse.bass as bass
import concourse.mybir as mybir
from concourse.bass_utils import run_bass_kernel_spmd

F32 = mybir.dt.float32
BF16 = mybir.dt.bfloat16
AF = mybir.ActivationFunctionType
ALU = mybir.AluOpType

S = 2048
D = 1024
NH = 16
FH = 2816
EPS = 1e-6
GROUPS = ((128, 1), (512, 4), (2048, 16))

COMPUTE = ("pe", "act", "dve", "pool")
NDSEM = 24


class Op:
    __slots__ = ("eng", "fn", "deps", "sig", "cnt", "idx", "dsem", "dval", "q")

    def __init__(self, eng, fn):
        self.eng = eng
        self.fn = fn
        self.deps = []
        self.sig = False
        self.cnt = 0
        self.dsem = None
        self.dval = 0


class Prog:
    def __init__(self, nc):
        self.nc = nc
        self.ops = []
        self.lastw = {}
        self.readers = {}
        self.ndma = {"sp": 0, "pq": 0}
        self.bank = 0
        self.bar = []
        self.reserved = set()
        self.last_eng = {}
        self.last_dma = {}

    def barrier(self):
        self.bar = [self.ops[i] for i in list(self.last_eng.values()) + list(self.last_dma.values())]

    def nb(self):
        while True:
            b = self.bank
            self.bank = (b + 1) % 8
            if b not in self.reserved:
                return b

    def op(self, eng, fn, reads=(), writes=()):
        o = Op(eng, fn)
        o.idx = len(self.ops)
        deps = set()
        for r in reads:
            w = self.lastw.get(r)
            if w is not None:
                deps.add(w)
        for w_ in writes:
            w = self.lastw.get(w_)
            if w is not None:
                deps.add(w)
            for rd in self.readers.get(w_, ()):
                deps.add(rd)
        o.deps = list(self.bar) + [self.ops[d] for d in sorted(deps)]
        for r in reads:
            self.readers.setdefault(r, []).append(o.idx)
        for w_ in writes:
            self.lastw[w_] = o.idx
            self.readers[w_] = []
        if eng in ("sp", "pq"):
            k = self.ndma[eng]
            self.ndma[eng] = k + 1
            o.q = k
            self.last_dma[(eng, k % NDSEM)] = o.idx
        else:
            self.last_eng[eng] = o.idx
        self.ops.append(o)
        return o

    def emit(self, final_wait_ops=()):
        nc = self.nc
        ops = self.ops
        for o in ops:
            for d in o.deps:
                if d.eng in COMPUTE:
                    if d.eng == "pe" and o.eng == "pe":
                        continue
                    d.sig = True
        cnt = {e: 0 for e in COMPUTE}
        for o in ops:
            if o.eng in COMPUTE and o.sig:
                cnt[o.eng] += 1
                o.cnt = cnt[o.eng]
        per = {e: [] for e in ("pe", "act", "dve", "pool", "sp")}
        for o in ops:
            per["pool" if o.eng == "pq" else o.eng].append(o)

        with contextlib.ExitStack() as st:
            psem = {e: st.enter_context(nc.semaphore("prog_" + e)) for e in COMPUTE}
            dsem = {q: [st.enter_context(nc.semaphore(f"d{q}{i}")) for i in range(NDSEM)]
                    for q in ("sp", "pq")}
            for o in ops:
                if o.eng in ("sp", "pq"):
                    o.dsem = dsem[o.eng][o.q % NDSEM]
                    o.dval = 16 * (o.q // NDSEM + 1)
            block = st.enter_context(nc.Block())

            def run(engname, handle):
                waited = {}
                for o in per[engname]:
                    if o.eng in ("sp", "pq") and o.q >= NDSEM:
                        key = (o.eng, o.q % NDSEM)
                        if waited.get(key, 0) < o.dval - 16:
                            waited[key] = o.dval - 16
                            handle.wait_ge(o.dsem, o.dval - 16)
                    for d in o.deps:
                        if d.eng in COMPUTE:
                            if d.eng == "pe" and o.eng == "pe":
                                continue
                            key = d.eng
                            val = d.cnt
                            sem = psem[d.eng]
                        else:
                            key = (d.eng, d.q % NDSEM)
                            val = d.dval
                            sem = d.dsem
                        if waited.get(key, 0) >= val:
                            continue
                        waited[key] = val
                        handle.wait_ge(sem, val)
                    ins = o.fn(handle)
                    if o.eng in ("sp", "pq"):
                        ins.then_inc(o.dsem, 16)
                    elif o.sig:
                        ins.then_inc(psem[o.eng], 1)
                if engname == "sp":
                    for o in final_wait_ops:
                        handle.wait_ge(o.dsem, o.dval)

            @block.sync
            def _(e):
                run("sp", e)

            @block.tensor
            def _(e):
                run("pe", e)

            @block.scalar
            def _(e):
                run("act", e)

            @block.vector
            def _(e):
                run("dve", e)

            @block.gpsimd
            def _(e):
                run("pool", e)


def alibi_slopes():
    return [2.0 ** (-8.0 * (i + 1) / NH) for i in range(NH)]


FFN_GROUPS = ((0, 5), (5, 10), (10, 14), (14, 18), (18, 22))
ARENA_WORDS = 28160
BIG = 1.0e30


def build(nseq=2, stages=("attn", "ffn0", "s5", "ffn1", "final")):
    nc = bass.Bass("TRN2", target_bir_lowering=False)
    T = nseq * S

    def din(name, shape):
        return nc.dram_tensor(name, list(shape), F32, kind="ExternalInput").ap()

    x = din("x", (T, D))
    gains_d = din("gains", (5, D))
    w_qkv = din("w_qkv", (D, 9216))
    w_ao = din("w_attn_out", (D, D))
    w_fi = [din("w_ffn_in0", (D, 2 * FH)), din("w_ffn_in1", (D, 2 * FH))]
    w_fo = [din("w_ffn_out0", (FH, D)), din("w_ffn_out1", (FH, D))]
    w_si_d = din("w_ssm_in", (D, D))
    a_re_d = din("a_re", (2, 64, 64))
    a_im_d = din("a_im", (2, 64, 64))
    log_dt_d = din("log_dt", (2, 64))
    b_re_d = din("b_re", (2, 64, 64, 16))
    b_im_d = din("b_im", (2, 64, 64, 16))
    c_re_d = din("c_re", (2, 64, 16, 64))
    c_im_d = din("c_im", (2, 64, 16, 64))
    d_skip_d = din("d_skip", (D,))
    w_glu_d = din("w_glu", (D, 2 * D))
    y = nc.dram_tensor("y", [T, D], F32, kind="ExternalOutput").ap()

    st = contextlib.ExitStack()
    with st:
        def sb(name, shape, dt):
            return st.enter_context(nc.sbuf_tensor(name, list(shape), dt))

        p = Prog(nc)
        ps = st.enter_context(nc.psum_tensor("ps", [128, 8, 512], F32))
        h = sb("h", (128, 8, S), F32)
        hn = sb("hn", (128, 8, S), BF16)
        ident = sb("ident", (128, 128), F32)
        ones_bf = sb("ones_bf", (128, 128), BF16)
        ones_f = sb("ones_f", (128, 128), F32)
        gains = sb("gains_sb", (128, 5, 8), F32)
        EPS_AP = sb("eps_ap", (128, 1), F32)
        arena = sb("arena", (128, ARENA_WORDS), F32)

        class Arena:
            def __init__(self):
                self.off = 0
                self.n = 0

            def reset(self):
                self.off = 0
                p.barrier()

            def alloc(self, shape, dt):
                nel = 1
                for d_ in shape[1:]:
                    nel *= d_
                words = nel if dt == F32 else (nel + 1) // 2
                words = (words + 7) // 8 * 8
                assert self.off + words <= ARENA_WORDS, (self.off, words)
                v = arena[:, self.off:self.off + words]
                self.off += words
                if dt != F32:
                    v = v.bitcast(dt)
                v = v[:, 0:nel]
                if len(shape) == 3:
                    v = v.rearrange("p (a b) -> p a b", a=shape[1])
                elif len(shape) == 4:
                    v = v.rearrange("p (a b c) -> p a b c", a=shape[1], b=shape[2])
                self.n += 1
                return v, ("ar", self.n)

        A = Arena()

        p.op("pool", lambda e: e.memset(ident[:], 0.0), writes=["ident"])
        p.op("pool", lambda e: e.affine_select(out=ident[:], in_=ident[:], pattern=[[1, 128]],
                                               compare_op=ALU.not_equal, fill=1.0, base=0,
                                               channel_multiplier=-1),
             reads=["ident"], writes=["ident"])
        p.op("pool", lambda e: e.memset(ones_bf[:], 1.0 / D), writes=["ones_bf"])
        p.op("pool", lambda e: e.memset(ones_f[:], 1.0), writes=["ones_f"])
        p.op("pool", lambda e: e.memset(EPS_AP[:], EPS), writes=["eps"])
        for g in range(5):
            p.op("sp", lambda e, g=g: e.dma_start(out=gains[:, g, :],
                                                  in_=gains_d[g].rearrange("(c p) -> p c", p=128),
                                                  allow_slow_non_contiguous=True),
                 writes=["gains"])

        def load_x(s):
            A.reset()
            xio = [A.alloc((128, D), F32) for i in range(2)]
            for tl in range(S // 128):
                xb, key = xio[tl % 2]
                t0 = s * S + tl * 128
                p.op("sp", lambda e, xb=xb, t0=t0: e.dma_start(out=xb, in_=x[t0:t0 + 128, :]), writes=[key])
                for half in range(2):
                    b = p.nb()
                    for j in range(4):
                        c = half * 4 + j
                        p.op("pe", lambda e, b=b, j=j, c=c, xb=xb: e.transpose(
                            out=ps[:, b, j * 128:(j + 1) * 128], in_=xb[:, c * 128:(c + 1) * 128], identity=ident[:]),
                            reads=[key, "ident"], writes=[("ps", b)])
                    dst = h[:, half * 4:half * 4 + 4, tl * 128:(tl + 1) * 128]
                    src = ps[:, b, :].rearrange("p (j t) -> p j t", j=4)
                    if half == 0:
                        p.op("act", lambda e, dst=dst, src=src: e.activation(out=dst, in_=src, func=AF.Copy),
                             reads=[("ps", b)], writes=[("h", tl // 4)])
                    else:
                        p.op("dve", lambda e, dst=dst, src=src: e.tensor_copy(out=dst, in_=src),
                             reads=[("ps", b)], writes=[("h", tl // 4)])

        def store_y(s, src_t):
            A.reset()
            xio = [A.alloc((128, D), F32) for i in range(2)]
            outs = []
            for tl in range(S // 128):
                xb, key = xio[tl % 2]
                t0 = s * S + tl * 128
                for half in range(2):
                    b = p.nb()
                    for j in range(4):
                        c = half * 4 + j
                        p.op("pe", lambda e, b=b, j=j, c=c, tl=tl: e.transpose(
                            out=ps[:, b, j * 128:(j + 1) * 128], in_=src_t[:, c, tl * 128:(tl + 1) * 128], identity=ident[:]),
                            reads=[("h", tl // 4), "ident"], writes=[("ps", b)])
                    dst = xb[:, half * 512:(half + 1) * 512]
                    if half == 0:
                        p.op("act", lambda e, dst=dst, b=b: e.activation(out=dst, in_=ps[:, b, :], func=AF.Copy),
                             reads=[("ps", b)], writes=[key])
                    else:
                        p.op("dve", lambda e, dst=dst, b=b: e.tensor_copy(out=dst, in_=ps[:, b, :]),
                             reads=[("ps", b)], writes=[key])
                outs.append(p.op("sp", lambda e, xb=xb, t0=t0: e.dma_start(out=y[t0:t0 + 128, :], in_=xb), reads=[key]))
            return outs

        def rms_norm(gi, out_t, out_key, in_place=False):
            A.reset()
            sq, sqk = A.alloc((128, 8, 512), BF16)
            rs = [A.alloc((128, 512), F32) for i in range(2)]
            for tt in range(4):
                sl = slice(tt * 512, (tt + 1) * 512)
                p.op("act", lambda e, sl=sl: e.activation(out=sq, in_=h[:, :, sl], func=AF.Square),
                     reads=[("h", tt)], writes=[sqk])
                b = p.nb()
                for c in range(8):
                    p.op("pe", lambda e, b=b, c=c: e.matmul(ps[:, b, :], ones_bf[:], sq[:, c, :], start=(c == 0), stop=(c == 7)),
                         reads=[sqk, "ones_bf"], writes=[("ps", b)])
                r, rk = rs[tt % 2]
                p.op("act", lambda e, r=r, b=b: e.activation(out=r, in_=ps[:, b, :], func=AF.Sqrt, bias=EPS_AP[:], scale=1.0),
                     reads=[("ps", b), "eps"], writes=[rk])
                p.op("dve", lambda e, r=r: e.reciprocal(out=r, in_=r), reads=[rk], writes=[rk])
                for c in range(8):
                    wr = [("h", tt)] if in_place else [(out_key, tt)]
                    p.op("dve", lambda e, c=c, sl=sl, r=r: e.scalar_tensor_tensor(
                        out=out_t[:, c, sl], in0=h[:, c, sl], scalar=gains[:, gi, c:c + 1], in1=r,
                        op0=ALU.mult, op1=ALU.mult),
                        reads=[("h", tt), rk, "gains"], writes=wr)

        def ffn(layer, gi):
            rms_norm(gi, hn, "hn")
            A.reset()
            NGM = max(c1 - c0 for c0, c1 in FFN_GROUPS)
            win = [A.alloc((128, 8, 2, NGM * 128), BF16) for i in range(2)]
            wout = [A.alloc((128, NGM, D), BF16) for i in range(2)]
            actb = [A.alloc((128, NGM, 512), BF16) for i in range(2)]
            sg = [A.alloc((128, 512), F32) for i in range(2)]
            wi_v = w_fi[layer].rearrange("(kc p) n -> p kc n", p=128)
            wo_v = w_fo[layer].rearrange("(hc p) n -> p hc n", p=128)
            nsg = 0
            nact = 0
            for gidx, (c0, c1) in enumerate(FFN_GROUPS):
                ng = c1 - c0
                wi, wik = win[gidx % 2]
                wo, wok = wout[gidx % 2]
                for gu in range(2):
                    for kc in range(8):
                        p.op("pq", lambda e, wi=wi, gu=gu, kc=kc, c0=c0, ng=ng: e.dma_start(
                            out=wi[:, kc, gu, 0:ng * 128],
                            in_=wi_v[:, kc, gu * FH + c0 * 128: gu * FH + c0 * 128 + ng * 128]),
                            writes=[wik])
                for j in range(ng):
                    p.op("pq", lambda e, wo=wo, j=j, c0=c0: e.dma_start(out=wo[:, j, :], in_=wo_v[:, c0 + j, :]),
                         writes=[wok])
                for tt in range(4):
                    sl = slice(tt * 512, (tt + 1) * 512)
                    ab, akey = actb[nact % 2]
                    nact += 1
                    for j in range(ng):
                        bg = p.nb()
                        bu = p.nb()
                        for gu, bb in ((0, bg), (1, bu)):
                            for kc in range(8):
                                p.op("pe", lambda e, bb=bb, gu=gu, kc=kc, j=j, wi=wi, sl=sl: e.matmul(
                                    ps[:, bb, :], wi[:, kc, gu, j * 128:(j + 1) * 128], hn[:, kc, sl],
                                    start=(kc == 0), stop=(kc == 7)),
                                    reads=[wik, ("hn", tt)], writes=[("ps", bb)])
                        sgb, skey = sg[nsg % 2]
                        nsg += 1
                        p.op("act", lambda e, sgb=sgb, bg=bg: e.activation(out=sgb, in_=ps[:, bg, :], func=AF.Silu),
                             reads=[("ps", bg)], writes=[skey])
                        p.op("dve", lambda e, ab=ab, j=j, sgb=sgb, bu=bu: e.tensor_tensor(
                            out=ab[:, j, :], in0=sgb, in1=ps[:, bu, :], op=ALU.mult),
                            reads=[skey, ("ps", bu)], writes=[akey])
                    for oc in range(8):
                        bo = p.nb()
                        for j in range(ng):
                            p.op("pe", lambda e, bo=bo, j=j, oc=oc, wo=wo, ab=ab, ng=ng: e.matmul(
                                ps[:, bo, :], wo[:, j, oc * 128:(oc + 1) * 128], ab[:, j, :],
                                start=(j == 0), stop=(j == ng - 1)),
                                reads=[wok, akey], writes=[("ps", bo)])
                        p.op("dve", lambda e, bo=bo, oc=oc, sl=sl: e.tensor_tensor(
                            out=h[:, oc, sl], in0=h[:, oc, sl], in1=ps[:, bo, :], op=ALU.add),
                            reads=[("ps", bo), ("h", tt)], writes=[("h", tt)])

        def attn():
            rms_norm(0, hn, "hn")
            A.reset()
            slopes = alibi_slopes()
            qT, qk_ = A.alloc((128, S), BF16)
            kT, kk_ = A.alloc((128, 4096), BF16)
            vaug, vk_ = A.alloc((128, 32, 256), BF16)
            oacc = [A.alloc((128, S), F32) for i in range(2)]
            oT, otk = A.alloc((128, 2, S), BF16)
            wqkv = [A.alloc((128, 8, 3, 128), BF16) for i in range(2)]
            wo_b = [A.alloc((128, 2, D), BF16) for i in range(2)]
            rmb = {k: A.alloc((128, 2, 2, 128), F32) for k in ("FI", "II", "IL", "BB")}
            tS = [A.alloc((128, 512), F32) for i in range(3)]
            pT = [A.alloc((128, 512), BF16) for i in range(3)]
            rec, reck = A.alloc((128, 512), F32)
            ri, rik = A.alloc((128, 2, 128), F32)
            mk, mkk = A.alloc((128, 2, 128), F32)

            p.op("pool", lambda e: e.iota(out=ri, pattern=[[128, 2], [-1, 128]], base=-64, channel_multiplier=1,
                                          allow_small_or_imprecise_dtypes=True), writes=[rik])
            p.op("act", lambda e: e.activation(out=ri, in_=ri, func=AF.Abs), reads=[rik], writes=[rik])
            p.op("dve", lambda e: e.tensor_scalar(out=mk, in0=ri, scalar1=-64.0, scalar2=0.0, op0=ALU.add, op1=ALU.max),
                 reads=[rik], writes=[mkk])
            p.op("dve", lambda e: e.tensor_scalar(out=mk, in0=mk, scalar1=BIG, scalar2=None, op0=ALU.mult),
                 reads=[mkk], writes=[mkk])
            p.op("dve", lambda e: e.tensor_tensor(out=ri, in0=ri, in1=mk, op=ALU.add), reads=[rik, mkk], writes=[rik])
            for k_, (t_, tk_) in rmb.items():
                for qi in range(2):
                    p.op("pool", lambda e, t_=t_, qi=qi: e.tensor_copy(out=t_[:, qi], in_=ri), reads=[rik], writes=[tk_])
            p.op("pool", lambda e: e.memset(rmb["FI"][0][0:64, 0, 0, :], BIG), writes=[rmb["FI"][1]])
            p.op("pool", lambda e: e.memset(rmb["IL"][0][64:128, 1, 1, :], BIG), writes=[rmb["IL"][1]])
            for qi in range(2):
                p.op("pool", lambda e, qi=qi: e.memset(rmb["BB"][0][0:64, qi, 0, :], BIG), writes=[rmb["BB"][1]])
                p.op("pool", lambda e, qi=qi: e.memset(rmb["BB"][0][64:128, qi, 1, :], BIG), writes=[rmb["BB"][1]])
            p.op("pool", lambda e: e.memset(vaug, 0.0), writes=[vk_])
            p.op("pool", lambda e: e.memset(vaug[:, :, 64:65], 1.0), writes=[vk_])
            p.op("pool", lambda e: e.memset(vaug[:, :, 128:129], 1.0), writes=[vk_])

            wq_v = w_qkv.rearrange("(kc p) n -> p kc n", p=128)
            wao_v = w_ao.rearrange("(hp p) n -> p hp n", p=128)
            it = 0
            for rnd in range(4):
                wo, wok = wo_b[rnd % 2]
                p.op("pq", lambda e, wo=wo, rnd=rnd: e.dma_start(out=wo, in_=wao_v[:, rnd * 2:rnd * 2 + 2, :]), writes=[wok])
                for hpi in range(2):
                    hp = rnd * 2 + hpi
                    for g, (window, d) in enumerate(GROUPS):
                        L = S // d
                        LB = L + 128
                        nq = L // 128
                        ntile = nq + 1
                        wq, wqk = wqkv[it % 2]
                        it += 1
                        for which in range(3):
                            col0 = g * 3072 + which * 1024 + hp * 128
                            p.op("pq", lambda e, wq=wq, which=which, col0=col0: e.dma_start(
                                out=wq[:, :, which, :], in_=wq_v[:, :, col0:col0 + 128]), writes=[wqk])
                        kTv = kT[:, 0:d * LB].rearrange("p (r m) -> p r m", r=d)
                        p.op("pool", lambda e, kTv=kTv: e.memset(kTv[:, :, 0:64], 0.0), writes=[kk_])
                        p.op("pool", lambda e, kTv=kTv, L=L: e.memset(kTv[:, :, 64 + L:128 + L], 0.0), writes=[kk_])
                        qTv = qT.rearrange("p (r l) -> p r l", r=d)
                        for which in range(2):
                            for tt in range(4):
                                b = p.nb()
                                for kc in range(8):
                                    p.op("pe", lambda e, b=b, kc=kc, wq=wq, which=which, tt=tt: e.matmul(
                                        ps[:, b, :], wq[:, kc, which, :], hn[:, kc, tt * 512:(tt + 1) * 512],
                                        start=(kc == 0), stop=(kc == 7)),
                                        reads=[wqk, ("hn", tt)], writes=[("ps", b)])
                                src = ps[:, b, :].rearrange("p (l r) -> p r l", r=d)
                                n_l = 512 // d
                                if which == 0:
                                    dst = qTv[:, :, tt * n_l:(tt + 1) * n_l]
                                    wkey = qk_
                                else:
                                    dst = kTv[:, :, 64 + tt * n_l:64 + (tt + 1) * n_l]
                                    wkey = kk_
                                p.op("act", lambda e, dst=dst, src=src: e.activation(out=dst, in_=src, func=AF.Copy),
                                     reads=[("ps", b)], writes=[wkey])
                        slot = 0
                        b = None
                        for r in range(d):
                            for m in range(ntile):
                                pos0 = max(0, 128 * m - 64)
                                pos1 = min(L, 128 * m + 64)
                                nv = pos1 - pos0
                                po = 64 if m == 0 else 0
                                if slot % 4 == 0:
                                    b = p.nb()
                                t_start = pos0 * d + r
                                t_stop = t_start + d * (nv - 1) + 1
                                for kc in range(8):
                                    p.op("pe", lambda e, b=b, kc=kc, wq=wq, po=po, nv=nv, t_start=t_start, t_stop=t_stop, d=d, sl_=slot % 4: e.matmul(
                                        ps[po:po + nv, b, sl_ * 128:(sl_ + 1) * 128], hn[:, kc, t_start:t_stop:d], wq[:, kc, 2, :],
                                        start=(kc == 0), stop=(kc == 7)),
                                        reads=[wqk] + [("hn", t_) for t_ in range(4)], writes=[("ps", b)])
                                tile_i = r * ntile + m
                                dst = vaug[po:po + nv, tile_i, :].rearrange("p (a b) -> p a b", a=4)[:, 0:4:3, :]
                                src = ps[po:po + nv, b, (slot % 4) * 128:(slot % 4 + 1) * 128].rearrange("p (a b) -> p a b", a=2)
                                p.op("act", lambda e, dst=dst, src=src: e.activation(out=dst, in_=src, func=AF.Copy),
                                     reads=[("ps", b)], writes=[vk_])
                                slot += 1
                        units = []
                        for e_ in range(2):
                            for quad in range(4):
                                for bp in range(2):
                                    units.append((e_, quad, bp))
                        pend = []
                        obank = {}

                        def ttype(qt):
                            if nq == 1:
                                return "B"
                            return "F" if qt == 0 else ("L" if qt == nq - 1 else "I")

                        def emit_pv(u, info):
                            e_, quad, bp = u
                            pt, ptk = info
                            if bp == 0:
                                obank[(e_, quad)] = p.nb()
                            bo = obank[(e_, quad)]
                            M = 65 if e_ == 0 else 128
                            c0 = 0 if e_ == 0 else 128
                            for qi in range(2):
                                n = quad * 4 + bp * 2 + qi
                                r, qt = divmod(n, nq)
                                for j in range(2):
                                    tile_i = r * ntile + qt + j
                                    p.op("pe", lambda e, bo=bo, M=M, c0=c0, tile_i=tile_i, pt=pt, qi=qi, j=j, bp=bp: e.matmul(
                                        ps[0:M, bo, (2 * bp + qi) * 128:(2 * bp + qi + 1) * 128],
                                        vaug[:, tile_i, c0:c0 + M], pt[:, (qi * 2 + j) * 128:(qi * 2 + j + 1) * 128],
                                        start=(j == 0), stop=(j == 1)),
                                        reads=[vk_, ptk], writes=[("ps", bo)])
                            if bp == 1:
                                oa, oak = oacc[e_]
                                if d == 1:
                                    dst = oa[0:M, quad * 512:(quad + 1) * 512]
                                    src = ps[0:M, bo, :]
                                elif d == 4:
                                    dst = oa[0:M, quad:S:4]
                                    src = ps[0:M, bo, :]
                                else:
                                    dst = oa[0:M, :].rearrange("p (l r) -> p r l", r=16)[:, quad * 4:quad * 4 + 4, :]
                                    src = ps[0:M, bo, :].rearrange("p (r l) -> p r l", r=4)
                                if g == 0:
                                    p.op("act", lambda e, dst=dst, src=src: e.activation(out=dst, in_=src, func=AF.Copy),
                                         reads=[("ps", bo)], writes=[oak])
                                else:
                                    p.op("dve", lambda e, dst=dst, src=src: e.tensor_tensor(out=dst, in0=src, in1=dst, op=ALU.add),
                                         reads=[("ps", bo), oak], writes=[oak])

                        for ui, u in enumerate(units):
                            e_, quad, bp = u
                            hd = 2 * hp + e_
                            cc = slopes[hd] * d
                            bS = p.nb()
                            types = ""
                            for qi in range(2):
                                n = quad * 4 + bp * 2 + qi
                                r, qt = divmod(n, nq)
                                types += ttype(qt)
                                for j in range(2):
                                    m = qt + j
                                    p.op("pe", lambda e, bS=bS, e_=e_, r=r, m=m, n=n, qi=qi, j=j, LB=LB: e.matmul(
                                        ps[:, bS, (qi * 2 + j) * 128:(qi * 2 + j + 1) * 128],
                                        kT[64 * e_:64 * e_ + 64, r * LB + 128 * m:r * LB + 128 * m + 128],
                                        qT[64 * e_:64 * e_ + 64, n * 128:(n + 1) * 128],
                                        start=True, stop=True),
                                        reads=[kk_, qk_], writes=[("ps", bS)])
                            rm, rmk = rmb[{"FI": "FI", "II": "II", "IL": "IL", "BB": "BB"}[types]]
                            ts_, tsk = tS[ui % 3]
                            pt, ptk = pT[ui % 3]
                            p.op("dve", lambda e, ts_=ts_, rm=rm, cc=cc, bS=bS: e.scalar_tensor_tensor(
                                out=ts_, in0=rm.rearrange("p a b c -> p (a b c)"), scalar=-8.0 * cc, in1=ps[:, bS, :],
                                op0=ALU.mult, op1=ALU.add),
                                reads=[rmk, ("ps", bS)], writes=[tsk])
                            p.op("act", lambda e, pt=pt, ts_=ts_: e.activation(out=pt, in_=ts_, func=AF.Exp, scale=0.125),
                                 reads=[tsk], writes=[ptk])
                            pend.append((u, (pt, ptk)))
                            if len(pend) > 2:
                                emit_pv(*pend.pop(0))
                        while pend:
                            emit_pv(*pend.pop(0))
                    for e_ in range(2):
                        oa, oak = oacc[e_]
                        row = 64 if e_ == 0 else 0
                        for tt in range(4):
                            sl = slice(tt * 512, (tt + 1) * 512)
                            b = p.nb()
                            p.op("pe", lambda e, b=b, row=row, oa=oa, sl=sl: e.matmul(
                                ps[:, b, :], ones_f[row:row + 1, :], oa[row:row + 1, sl], start=True, stop=True),
                                reads=[oak, "ones_f"], writes=[("ps", b)])
                            pr = slice(64 * e_, 64 * e_ + 64)
                            p.op("dve", lambda e, b=b, pr=pr: e.reciprocal(out=rec[pr, :], in_=ps[pr, b, :]),
                                 reads=[("ps", b)], writes=[reck])
                            p.op("dve", lambda e, pr=pr, oa=oa, sl=sl, hpi=hpi: e.tensor_tensor(
                                out=oT[pr, hpi, sl], in0=oa[pr, sl], in1=rec[pr, :], op=ALU.mult),
                                reads=[reck, oak], writes=[otk])
                for tt in range(4):
                    sl = slice(tt * 512, (tt + 1) * 512)
                    for oc in range(8):
                        bo = p.nb()
                        for hpi in range(2):
                            p.op("pe", lambda e, bo=bo, hpi=hpi, oc=oc, wo=wo, sl=sl: e.matmul(
                                ps[:, bo, :], wo[:, hpi, oc * 128:(oc + 1) * 128], oT[:, hpi, sl],
                                start=(hpi == 0), stop=(hpi == 1)),
                                reads=[wok, otk], writes=[("ps", bo)])
                        p.op("dve", lambda e, bo=bo, oc=oc, sl=sl: e.tensor_tensor(
                            out=h[:, oc, sl], in0=h[:, oc, sl], in1=ps[:, bo, :], op=ALU.add),
                            reads=[("ps", bo), ("h", tt)], writes=[("h", tt)])


        def s5():
            rms_norm(2, hn, "hn")
            A.reset()
            uT, utk = A.alloc((128, 8, S), BF16)
            xb = [[A.alloc((128, S), F32) for _ in range(2)] for _ in range(2)]
            xbf = [A.alloc((128, S), BF16) for _ in range(2)]
            ug, ugk = A.alloc((128, S), BF16)
            def T64():
                return A.alloc((128, 64), F32)
            are, aim, dtt, th, mag, sn, cs, t1, t2, fr, fi, inv = [T64() for _ in range(12)]
            pw = [A.alloc((128, 11, 64), F32) for _ in range(3)]
            bpl = [(xb[0][r_][0][:, 0:1024].rearrange("p (g c) -> p g c", c=16), ("bpl", r_)) for r_ in range(2)]
            btmp = [(xb[1][r_][0][:, 0:1024].rearrange("p (g c) -> p g c", c=16), ("btmp", r_)) for r_ in range(2)]
            bcl = [A.alloc((128, 8, 128), BF16) for _ in range(2)]
            cnat, cnk = xb[0][0][0][:, 1024:2048].rearrange("p (j d q) -> p j d q", j=8, d=2), "cnat"
            cpl = [A.alloc((128, 8, 128), BF16) for _ in range(2)]
            cpad = [A.alloc((128, 128), BF16) for _ in range(4)]
            anat, ank = A.alloc((128, 2, 64), F32)
            colmask, cmk = A.alloc((128, 8, 128), BF16)
            rowmask, rmk_ = A.alloc((128, 8), F32)
            e01, e01k = A.alloc((128, 2, 128), F32)
            ldt, ldk = A.alloc((128, 2, 64), F32)
            dsk, dskk = A.alloc((128, 8), F32)
            ddiag, ddk = A.alloc((128, 128), BF16)
            wsi = [A.alloc((128, 8, 128), BF16)] * 2
            wgl = A.alloc((128, 8, 2, 128), BF16)
            gx = [None, (xb[1][0][0][:, 0:512], "gx1"), (xb[1][1][0][:, 0:512], "gx2")]

            def tt_(eng, o, a, b_, op):
                p.op(eng, lambda e: e.tensor_tensor(out=o[0], in0=a[0], in1=b_[0], op=op), reads=[a[1], b_[1]], writes=[o[1]])

            def ts_(eng, o, a, s1, s2, op0, op1=None):
                if op1 is None:
                    p.op(eng, lambda e: e.tensor_scalar(out=o[0], in0=a[0], scalar1=s1, scalar2=None, op0=op0), reads=[a[1]], writes=[o[1]])
                else:
                    p.op(eng, lambda e: e.tensor_scalar(out=o[0], in0=a[0], scalar1=s1, scalar2=s2, op0=op0, op1=op1), reads=[a[1]], writes=[o[1]])

            def act_(o, a, func, scale=1.0):
                p.op("act", lambda e: e.activation(out=o[0], in_=a[0], func=func, scale=scale), reads=[a[1]], writes=[o[1]])

            p.op("pool", lambda e: e.memset(colmask, 0.0), writes=[cmk])
            for g8 in range(8):
                p.op("pool", lambda e, g8=g8: e.memset(colmask[:, g8, g8 * 16:(g8 + 1) * 16], 1.0), writes=[cmk])
            p.op("pool", lambda e: e.memset(rowmask, 1.0), writes=[rmk_])
            p.op("pool", lambda e: e.affine_select(out=rowmask, in_=rowmask, pattern=[[-16, 8]], compare_op=ALU.is_ge,
                                                   fill=0.0, base=0, channel_multiplier=1), reads=[rmk_], writes=[rmk_])
            p.op("pool", lambda e: e.affine_select(out=rowmask, in_=rowmask, pattern=[[16, 8]], compare_op=ALU.is_ge,
                                                   fill=0.0, base=15, channel_multiplier=-1), reads=[rmk_], writes=[rmk_])
            p.op("pool", lambda e: e.memset(e01, 0.0), writes=[e01k])
            p.op("pool", lambda e: e.memset(e01[0:1, 0, 0:64], 1.0), writes=[e01k])
            p.op("pool", lambda e: e.memset(e01[0:1, 1, 64:128], 1.0), writes=[e01k])
            p.op("sp", lambda e: e.dma_start(out=dsk, in_=d_skip_d.rearrange("(c p) -> p c", p=128), allow_slow_non_contiguous=True), writes=[dskk])
            for d_ in range(2):
                p.op("sp", lambda e, d_=d_: e.dma_start(out=ldt[0:1, d_, :], in_=log_dt_d[d_:d_ + 1, :]), writes=[ldk])
            b0 = p.nb()
            for d_ in range(2):
                p.op("pe", lambda e, d_=d_, b0=b0: e.matmul(ps[:, b0, 0:64], e01[0:1, d_, :], ldt[0:1, d_, :], start=(d_ == 0), stop=(d_ == 1)),
                     reads=[e01k, ldk], writes=[("ps", b0)])
            p.op("act", lambda e, b0=b0: e.activation(out=dtt[0], in_=ps[:, b0, 0:64], func=AF.Exp), reads=[("ps", b0)], writes=[dtt[1]])
            for src_d, dst in ((a_re_d, are), (a_im_d, aim)):
                p.op("sp", lambda e, src_d=src_d: e.dma_start(out=anat[0:64], in_=src_d.rearrange("d g p -> g d p")), writes=[ank])
                b0 = p.nb()
                p.op("pe", lambda e, b0=b0: e.transpose(out=ps[:, b0, 0:64], in_=anat[0:64].rearrange("g d p -> g (d p)"), identity=ident[0:64, 0:64]),
                     reads=[ank, "ident"], writes=[("ps", b0)])
                p.op("act", lambda e, b0=b0, dst=dst: e.activation(out=dst[0], in_=ps[:, b0, 0:64], func=AF.Copy), reads=[("ps", b0)], writes=[dst[1]])
            tt_("dve", t1, are, dtt, ALU.mult)
            act_(mag, t1, AF.Exp)
            tt_("dve", th, aim, dtt, ALU.mult)
            act_(sn, th, AF.Sin, 1.0 / 16.0)
            tt_("dve", cs, sn, sn, ALU.mult)
            ts_("dve", cs, cs, -2.0, 1.0, ALU.mult, ALU.add)
            act_(sn, th, AF.Sin, 1.0 / 8.0)
            for _ in range(3):
                tt_("dve", t1, sn, cs, ALU.mult)
                tt_("dve", t2, sn, sn, ALU.mult)
                ts_("dve", sn, t1, 2.0, None, ALU.mult)
                ts_("dve", cs, t2, -2.0, 1.0, ALU.mult, ALU.add)
            lr = (pw[0][0][:, 0, :], pw[0][1])
            li = (pw[1][0][:, 0, :], pw[1][1])
            tt_("dve", lr, mag, cs, ALU.mult)
            tt_("dve", li, mag, sn, ALU.mult)
            tt_("dve", t1, are, are, ALU.mult)
            tt_("dve", t2, aim, aim, ALU.mult)
            tt_("dve", inv, t1, t2, ALU.add)
            p.op("dve", lambda e: e.reciprocal(out=inv[0], in_=inv[0]), reads=[inv[1]], writes=[inv[1]])
            ts_("dve", mag, lr, -1.0, None, ALU.add)
            tt_("dve", t1, mag, are, ALU.mult)
            tt_("dve", t2, li, aim, ALU.mult)
            tt_("dve", t1, t1, t2, ALU.add)
            tt_("dve", fr, t1, inv, ALU.mult)
            tt_("dve", t1, li, are, ALU.mult)
            tt_("dve", t2, mag, aim, ALU.mult)
            tt_("dve", t1, t1, t2, ALU.subtract)
            tt_("dve", fi, t1, inv, ALU.mult)
            for k in range(1, 11):
                pr_ = (pw[0][0][:, k - 1, :], pw[0][1]); pi_ = (pw[1][0][:, k - 1, :], pw[1][1])
                nr_ = (pw[0][0][:, k, :], pw[0][1]); ni_ = (pw[1][0][:, k, :], pw[1][1])
                tt_("dve", t1, pr_, pr_, ALU.mult)
                tt_("dve", t2, pi_, pi_, ALU.mult)
                tt_("dve", nr_, t1, t2, ALU.subtract)
                tt_("dve", t1, pr_, pi_, ALU.mult)
                ts_("dve", ni_, t1, 2.0, None, ALU.mult)
            ts_("dve", pw[2], pw[1], -1.0, None, ALU.mult)
            for ri, src_d in enumerate((b_re_d, b_im_d)):
                for d_ in range(2):
                    p.op("sp", lambda e, ri=ri, d_=d_, src_d=src_d: e.dma_start(
                        out=bpl[ri][0][64 * d_:64 * d_ + 64], in_=src_d[d_].rearrange("g p c -> p g c")), writes=[bpl[ri][1]])
            def cmul(o, a, f_):
                for c_ in range(16):
                    p.op("dve", lambda e, o=o, a=a, f_=f_, c_=c_: e.tensor_tensor(out=o[0][:, :, c_], in0=a[0][:, :, c_], in1=f_[0], op=ALU.mult),
                         reads=[a[1], f_[1]], writes=[o[1]])
            cmul(btmp[0], bpl[0], fr)
            cmul(btmp[1], bpl[1], fi)
            tt_("dve", btmp[0], btmp[0], btmp[1], ALU.subtract)
            cmul(btmp[1], bpl[1], fr)
            cmul(bpl[1], bpl[0], fi)
            tt_("dve", btmp[1], btmp[1], bpl[1], ALU.add)
            for ri in range(2):
                for half in range(2):
                    b0 = p.nb()
                    for jj in range(4):
                        j = half * 4 + jj
                        p.op("pe", lambda e, b0=b0, jj=jj, j=j, ri=ri: e.transpose(
                            out=ps[:, b0, jj * 128:(jj + 1) * 128],
                            in_=btmp[ri][0][:, j * 8:(j + 1) * 8, :].rearrange("p g c -> p (g c)"), identity=ident[:]),
                            reads=[btmp[ri][1], "ident"], writes=[("ps", b0)])
                    p.op("act", lambda e, b0=b0, half=half, ri=ri: e.activation(
                        out=bcl[ri][0][:, half * 4:half * 4 + 4, :], in_=ps[:, b0, :].rearrange("p (j q) -> p j q", j=4), func=AF.Copy),
                        reads=[("ps", b0)], writes=[bcl[ri][1]])
            for ri, src_d in enumerate((c_re_d, c_im_d)):
                for d_ in range(2):
                    p.op("sp", lambda e, d_=d_, src_d=src_d: e.dma_start(
                        out=cnat[:, :, d_, :], in_=src_d[d_].rearrange("(j g) c p -> (g c) j p", g=8)), writes=[cnk])
                for half in range(2):
                    b0 = p.nb()
                    for jj in range(4):
                        j = half * 4 + jj
                        p.op("pe", lambda e, b0=b0, jj=jj, j=j: e.transpose(
                            out=ps[:, b0, jj * 128:(jj + 1) * 128], in_=cnat[:, j, :, :].rearrange("p d q -> p (d q)"), identity=ident[:]),
                            reads=[cnk, "ident"], writes=[("ps", b0)])
                    p.op("act", lambda e, b0=b0, half=half, ri=ri: e.activation(
                        out=cpl[ri][0][:, half * 4:half * 4 + 4, :], in_=ps[:, b0, :].rearrange("p (j q) -> p j q", j=4),
                        func=AF.Copy, scale=(1.0 if ri == 0 else -1.0)),
                        reads=[("ps", b0)], writes=[cpl[ri][1]])
            wsi_v = w_si_d.rearrange("(kc p) n -> p kc n", p=128)
            for j in range(8):
                ws, wsk = wsi[j % 2]
                p.op("pq", lambda e, ws=ws, j=j: e.dma_start(out=ws, in_=wsi_v[:, :, j * 128:(j + 1) * 128]), writes=[wsk])
                for tt in range(4):
                    b0 = p.nb()
                    for kc in range(8):
                        p.op("pe", lambda e, b0=b0, kc=kc, ws=ws, tt=tt: e.matmul(
                            ps[:, b0, :], ws[:, kc, :], hn[:, kc, tt * 512:(tt + 1) * 512], start=(kc == 0), stop=(kc == 7)),
                            reads=[wsk, ("hn", tt)], writes=[("ps", b0)])
                    p.op("act", lambda e, b0=b0, j=j, tt=tt: e.activation(out=uT[:, j, tt * 512:(tt + 1) * 512], in_=ps[:, b0, :], func=AF.Copy),
                         reads=[("ps", b0)], writes=[utk])
            ncp = 0
            p.barrier()
            for j in range(8):
                ybanks = [p.nb() for _ in range(4)]
                for b0 in ybanks:
                    p.reserved.add(b0)
                p.op("dve", lambda e, j=j: e.tensor_scalar(out=ddiag, in0=ident[:], scalar1=dsk[:, j:j + 1], scalar2=None, op0=ALU.mult),
                     reads=["ident", dskk], writes=[ddk])
                for tt in range(4):
                    p.op("pe", lambda e, tt=tt, j=j, yb=ybanks[tt]: e.matmul(
                        ps[:, yb, :], ddiag, uT[:, j, tt * 512:(tt + 1) * 512], start=True, stop=False),
                        reads=[ddk, utk], writes=[("ps", ybanks[tt])])
                for g8 in range(8):
                    g = j * 8 + g8
                    p.op("dve", lambda e, j=j, g8=g8: e.tensor_scalar(out=ug, in0=uT[:, j, :], scalar1=rowmask[:, g8:g8 + 1], scalar2=None, op0=ALU.mult),
                         reads=[utk, rmk_], writes=[ugk])
                    for ri in range(2):
                        for tt in range(4):
                            b0 = p.nb()
                            p.op("pe", lambda e, b0=b0, ri=ri, j=j, tt=tt: e.matmul(
                                ps[:, b0, :], bcl[ri][0][:, j, :], ug[:, tt * 512:(tt + 1) * 512], start=True, stop=True),
                                reads=[bcl[ri][1], ugk], writes=[("ps", b0)])
                            p.op("act", lambda e, b0=b0, ri=ri, tt=tt: e.activation(
                                out=xb[0][ri][0][:, tt * 512:(tt + 1) * 512], in_=ps[:, b0, :], func=AF.Copy),
                                reads=[("ps", b0)], writes=[xb[0][ri][1]])
                    cur = 0
                    for k in range(11):
                        s_ = 1 << k
                        last = (k == 10)
                        src = xb[cur]
                        dst = xbf if last else xb[1 - cur]
                        ar_ = pw[0][0][:, k, g:g + 1]
                        ai_ = pw[1][0][:, k, g:g + 1]
                        nai_ = pw[2][0][:, k, g:g + 1]
                        for d_ in range(2):
                            P_ = slice(64 * d_, 64 * d_ + 64)
                            if d_ == 0:
                                o_ = slice(s_, S); i_ = slice(0, S - s_); keep = slice(0, s_)
                            else:
                                o_ = slice(0, S - s_); i_ = slice(s_, S); keep = slice(S - s_, S)
                            rd = [src[0][1], src[1][1], pw[0][1], pw[1][1], pw[2][1]]
                            p.op("dve", lambda e, P_=P_, o_=o_, i_=i_, src=src, dst=dst, ar_=ar_: e.scalar_tensor_tensor(
                                out=dst[0][0][P_, o_], in0=src[0][0][P_, i_], scalar=ar_[P_], in1=src[0][0][P_, o_],
                                op0=ALU.mult, op1=ALU.add), reads=rd, writes=[dst[0][1]])
                            p.op("dve", lambda e, P_=P_, o_=o_, i_=i_, src=src, dst=dst, nai_=nai_: e.scalar_tensor_tensor(
                                out=dst[0][0][P_, o_], in0=src[1][0][P_, i_], scalar=nai_[P_], in1=dst[0][0][P_, o_],
                                op0=ALU.mult, op1=ALU.add), reads=rd + [dst[0][1]], writes=[dst[0][1]])
                            p.op("dve", lambda e, P_=P_, o_=o_, i_=i_, src=src, dst=dst, ai_=ai_: e.scalar_tensor_tensor(
                                out=dst[1][0][P_, o_], in0=src[0][0][P_, i_], scalar=ai_[P_], in1=src[1][0][P_, o_],
                                op0=ALU.mult, op1=ALU.add), reads=rd, writes=[dst[1][1]])
                            p.op("dve", lambda e, P_=P_, o_=o_, i_=i_, src=src, dst=dst, ar_=ar_: e.scalar_tensor_tensor(
                                out=dst[1][0][P_, o_], in0=src[1][0][P_, i_], scalar=ar_[P_], in1=dst[1][0][P_, o_],
                                op0=ALU.mult, op1=ALU.add), reads=rd + [dst[1][1]], writes=[dst[1][1]])
                            for ri in range(2):
                                p.op("act", lambda e, P_=P_, keep=keep, src=src, dst=dst, ri=ri: e.activation(
                                    out=dst[ri][0][P_, keep], in_=src[ri][0][P_, keep], func=AF.Copy),
                                    reads=[src[ri][1]], writes=[dst[ri][1]])
                        cur = 1 - cur
                    for ri in range(2):
                        cp, cpk = cpad[ncp % 4]
                        ncp += 1
                        p.op("pool", lambda e, cp=cp, ri=ri, j=j, g8=g8: e.tensor_tensor(out=cp, in0=cpl[ri][0][:, j, :], in1=colmask[:, g8, :], op=ALU.mult),
                             reads=[cpl[ri][1], cmk], writes=[cpk])
                        for tt in range(4):
                            lastmm = (g8 == 7 and ri == 1)
                            p.op("pe", lambda e, cp=cp, ri=ri, tt=tt, yb=ybanks[tt], lastmm=lastmm: e.matmul(
                                ps[:, yb, :], cp, xbf[ri][0][:, tt * 512:(tt + 1) * 512], start=False, stop=lastmm),
                                reads=[cpk, xbf[ri][1]], writes=[("ps", ybanks[tt])])
                for tt in range(4):
                    yb = ybanks[tt]
                    sl = slice(tt * 512, (tt + 1) * 512)
                    g1, g2 = gx[1], gx[2]
                    p.op("act", lambda e, yb=yb: e.activation(out=gx[1][0], in_=ps[:, yb, :], func=AF.Copy), reads=[("ps", yb)], writes=[g1[1]])
                    tt_("dve", g2, g1, g1, ALU.mult)
                    ts_("dve", g2, g2, 0.044715, 1.0, ALU.mult, ALU.add)
                    tt_("dve", g2, g2, g1, ALU.mult)
                    act_(g2, g2, AF.Sigmoid, 1.5957691216057308)
                    p.op("dve", lambda e, j=j, sl=sl: e.tensor_tensor(out=hn[:, j, sl], in0=gx[1][0], in1=gx[2][0], op=ALU.mult),
                         reads=[g1[1], g2[1]], writes=[("hn", tt)])
                for b0 in ybanks:
                    p.reserved.discard(b0)
                p.barrier()
            wg, wgk = wgl
            wgl_v = w_glu_d.rearrange("(kc p) n -> p kc n", p=128)
            for oc in range(8):
                for gu in range(2):
                    p.op("pq", lambda e, oc=oc, gu=gu: e.dma_start(out=wg[:, :, gu, :], in_=wgl_v[:, :, gu * D + oc * 128: gu * D + (oc + 1) * 128]), writes=[wgk])
                for tt in range(4):
                    sl = slice(tt * 512, (tt + 1) * 512)
                    ba = p.nb(); bg = p.nb()
                    for gu, bb in ((0, ba), (1, bg)):
                        for kc in range(8):
                            p.op("pe", lambda e, bb=bb, gu=gu, kc=kc, sl=sl: e.matmul(
                                ps[:, bb, :], wg[:, kc, gu, :], hn[:, kc, sl], start=(kc == 0), stop=(kc == 7)),
                                reads=[wgk, ("hn", tt)], writes=[("ps", bb)])
                    p.op("act", lambda e, bg=bg: e.activation(out=gx[1][0], in_=ps[:, bg, :], func=AF.Sigmoid), reads=[("ps", bg)], writes=[gx[1][1]])
                    p.op("dve", lambda e, ba=ba: e.tensor_tensor(out=gx[2][0], in0=gx[1][0], in1=ps[:, ba, :], op=ALU.mult),
                         reads=[gx[1][1], ("ps", ba)], writes=[gx[2][1]])
                    p.op("dve", lambda e, oc=oc, sl=sl: e.tensor_tensor(out=h[:, oc, sl], in0=h[:, oc, sl], in1=gx[2][0], op=ALU.add),
                         reads=[gx[2][1], ("h", tt)], writes=[("h", tt)])

        outs = []
        for s in range(nseq):
            load_x(s)
            if "attn" in stages:
                attn()
            if "ffn0" in stages:
                ffn(0, 1)
            if "s5" in stages:
                s5()
            if "ffn1" in stages:
                ffn(1, 3)
            if "final" in stages:
                rms_norm(4, h, "h", in_place=True)
            outs += store_y(s, h)
        p.emit(final_wait_ops=outs)
    return nc


def make_in_map(P, xs):
    gains = np.stack([P["attn_norm"][0], P["ffn_norm"][0], P["ssm_norm"][0], P["ffn_norm"][1],
                      P["final_norm"]]).astype(np.float32)
    return {
        "x": np.ascontiguousarray(xs.reshape(-1, D)),
        "gains": gains,
        "w_qkv": np.ascontiguousarray(P["w_qkv"][0]),
        "w_attn_out": np.ascontiguousarray(P["w_attn_out"][0]),
        "w_ffn_in0": np.ascontiguousarray(P["w_ffn_in"][0]),
        "w_ffn_in1": np.ascontiguousarray(P["w_ffn_in"][1]),
        "w_ffn_out0": np.ascontiguousarray(P["w_ffn_out"][0]),
        "w_ffn_out1": np.ascontiguousarray(P["w_ffn_out"][1]),
        "w_ssm_in": np.ascontiguousarray(P["w_ssm_in"][0]),
        "a_re": np.ascontiguousarray(P["a_re"][0]),
        "a_im": np.ascontiguousarray(P["a_im"][0]),
        "log_dt": np.ascontiguousarray(P["log_dt"][0]),
        "b_re": np.ascontiguousarray(P["b_re"][0]),
        "b_im": np.ascontiguousarray(P["b_im"][0]),
        "c_re": np.ascontiguousarray(P["c_re"][0]),
        "c_im": np.ascontiguousarray(P["c_im"][0]),
        "d_skip": np.ascontiguousarray(P["d_skip"][0]),
        "w_glu": np.ascontiguousarray(P["w_glu"][0]),
    }


def kernel(**inputs):
    P = {k: np.asarray(v) for k, v in inputs.items()}
    x = P["x"]
    nc = build(nseq=2)
    in_maps = [make_in_map(P, x[2 * c:2 * c + 2]) for c in range(8)]
    res = run_bass_kernel_spmd(nc, in_maps, core_ids=list(range(8)))
    out = np.concatenate([np.asarray(r["y"]).reshape(2, S, D) for r in res.results], axis=0)
    return out.astype(np.float32)
```

```python
import contextlib
import math
import numpy as np
import concourse.bass as bass
import concourse.mybir as mybir
from concourse.bass_utils import run_bass_kernel_spmd

F32 = mybir.dt.float32
BF16 = mybir.dt.bfloat16
AF = mybir.ActivationFunctionType
ALU = mybir.AluOpType

S = 2048
D = 1024
NH = 16
FH = 2816
EPS = 1e-6
GROUPS = ((128, 1), (512, 4), (2048, 16))

COMPUTE = ("pe", "act", "dve", "pool")
NDSEM = 24
EPOCH = 12000


class Op:
    __slots__ = ("eng", "fn", "deps", "sig", "cnt", "idx", "dsem", "dval", "q")

    def __init__(self, eng, fn):
        self.eng = eng
        self.fn = fn
        self.deps = []
        self.sig = False
        self.cnt = 0
        self.dsem = None
        self.dval = 0


class Prog:
    def __init__(self, nc):
        self.nc = nc
        self.ops = []
        self.lastw = {}
        self.readers = {}
        self.ndma = {"sp": 0, "pq": 0}
        self.bank = 0
        self.bar = []
        self.reserved = set()
        self.last_eng = {}
        self.last_dma = {}

    def barrier(self):
        self.bar = [self.ops[i] for i in list(self.last_eng.values()) + list(self.last_dma.values())]

    def nb(self):
        while True:
            b = self.bank
            self.bank = (b + 1) % 8
            if b not in self.reserved:
                return b

    def op(self, eng, fn, reads=(), writes=()):
        o = Op(eng, fn)
        o.idx = len(self.ops)
        deps = set()
        for r in reads:
            w = self.lastw.get(r)
            if w is not None:
                deps.add(w)
        for w_ in writes:
            w = self.lastw.get(w_)
            if w is not None:
                deps.add(w)
            for rd in self.readers.get(w_, ()):
                deps.add(rd)
        o.deps = list(self.bar) + [self.ops[d] for d in sorted(deps)]
        for r in reads:
            self.readers.setdefault(r, []).append(o.idx)
        for w_ in writes:
            self.lastw[w_] = o.idx
            self.readers[w_] = []
        if eng in ("sp", "pq"):
            k = self.ndma[eng]
            self.ndma[eng] = k + 1
            o.q = k
            self.last_dma[(eng, k % NDSEM)] = o.idx
        else:
            self.last_eng[eng] = o.idx
        self.ops.append(o)
        return o

    def emit(self, final_wait_ops=()):
        nc = self.nc
        ops = self.ops
        for o in ops:
            for d in o.deps:
                if d.eng in COMPUTE:
                    if d.eng == "pe" and o.eng == "pe":
                        continue
                    d.sig = True
        cnt = {e: 0 for e in COMPUTE}
        for o in ops:
            if o.eng in COMPUTE and o.sig:
                cnt[o.eng] += 1
                o.cnt = cnt[o.eng]
        per = {e: [] for e in ("pe", "act", "dve", "pool", "sp")}
        for o in ops:
            per["pool" if o.eng == "pq" else o.eng].append(o)

        with contextlib.ExitStack() as st:
            psem = {e: [st.enter_context(nc.semaphore(f"prog_{e}{k}")) for k in range(cnt[e] // EPOCH + 1)]
                    for e in COMPUTE}
            dsem = {q: [st.enter_context(nc.semaphore(f"d{q}{i}")) for i in range(NDSEM)]
                    for q in ("sp", "pq")}
            for o in ops:
                if o.eng in ("sp", "pq"):
                    o.dsem = dsem[o.eng][o.q % NDSEM]
                    o.dval = 16 * (o.q // NDSEM + 1)
            block = st.enter_context(nc.Block())

            def run(engname, handle):
                waited = {}
                for o in per[engname]:
                    if o.eng in ("sp", "pq") and o.q >= NDSEM:
                        key = (o.eng, o.q % NDSEM)
                        if waited.get(key, 0) < o.dval - 16:
                            waited[key] = o.dval - 16
                            handle.wait_ge(o.dsem, o.dval - 16)
                    for d in o.deps:
                        if d.eng in COMPUTE:
                            if d.eng == "pe" and o.eng == "pe":
                                continue
                            ep = (d.cnt - 1) // EPOCH
                            key = (d.eng, ep)
                            val = d.cnt - ep * EPOCH
                            sem = psem[d.eng][ep]
                        else:
                            key = (d.eng, d.q % NDSEM)
                            val = d.dval
                            sem = d.dsem
                        if waited.get(key, 0) >= val:
                            continue
                        waited[key] = val
                        handle.wait_ge(sem, val)
                    ins = o.fn(handle)
                    if o.eng in ("sp", "pq"):
                        ins.then_inc(o.dsem, 16)
                    elif o.sig:
                        ins.then_inc(psem[o.eng][(o.cnt - 1) // EPOCH], 1)
                if engname == "sp":
                    for o in final_wait_ops:
                        handle.wait_ge(o.dsem, o.dval)

            @block.sync
            def _(e):
                run("sp", e)

            @block.tensor
            def _(e):
                run("pe", e)

            @block.scalar
            def _(e):
                run("act", e)

            @block.vector
            def _(e):
                run("dve", e)

            @block.gpsimd
            def _(e):
                run("pool", e)


def alibi_slopes():
    return [2.0 ** (-8.0 * (i + 1) / NH) for i in range(NH)]


FFN_GROUPS = ((0, 5), (5, 10), (10, 14), (14, 18), (18, 22))
ARENA_WORDS = 28160
BIG = 1.0e30


def build(nseq=2, stages=("attn", "ffn0", "s5", "ffn1", "final")):
    nc = bass.Bass("TRN2", target_bir_lowering=False)
    T = nseq * S

    def din(name, shape):
        return nc.dram_tensor(name, list(shape), F32, kind="ExternalInput").ap()

    x = din("x", (T, D))
    gains_d = din("gains", (5, D))
    w_qkv = din("w_qkv", (D, 9216))
    w_ao = din("w_attn_out", (D, D))
    w_fi = [din("w_ffn_in0", (D, 2 * FH)), din("w_ffn_in1", (D, 2 * FH))]
    w_fo = [din("w_ffn_out0", (FH, D)), din("w_ffn_out1", (FH, D))]
    w_si_d = din("w_ssm_in", (D, D))
    a_re_d = din("a_re", (2, 64, 64))
    a_im_d = din("a_im", (2, 64, 64))
    log_dt_d = din("log_dt", (2, 64))
    b_re_d = din("b_re", (2, 64, 64, 16))
    b_im_d = din("b_im", (2, 64, 64, 16))
    c_re_d = din("c_re", (2, 64, 16, 64))
    c_im_d = din("c_im", (2, 64, 16, 64))
    d_skip_d = din("d_skip", (D,))
    w_glu_d = din("w_glu", (D, 2 * D))
    y = nc.dram_tensor("y", [T, D], F32, kind="ExternalOutput").ap()

    st = contextlib.ExitStack()
    with st:
        def sb(name, shape, dt):
            return st.enter_context(nc.sbuf_tensor(name, list(shape), dt))

        p = Prog(nc)
        ps = st.enter_context(nc.psum_tensor("ps", [128, 8, 512], F32))
        h = sb("h", (128, 8, S), F32)
        hn = sb("hn", (128, 8, S), BF16)
        ident = sb("ident", (128, 128), F32)
        ones_bf = sb("ones_bf", (128, 128), BF16)
        ones_f = sb("ones_f", (128, 128), F32)
        gains = sb("gains_sb", (128, 5, 8), F32)
        EPS_AP = sb("eps_ap", (128, 1), F32)
        arena = sb("arena", (128, ARENA_WORDS), F32)

        class Arena:
            def __init__(self):
                self.off = 0
                self.n = 0

            def reset(self):
                self.off = 0
                p.barrier()

            def alloc(self, shape, dt):
                nel = 1
                for d_ in shape[1:]:
                    nel *= d_
                words = nel if dt == F32 else (nel + 1) // 2
                words = (words + 7) // 8 * 8
                assert self.off + words <= ARENA_WORDS, (self.off, words)
                v = arena[:, self.off:self.off + words]
                self.off += words
                if dt != F32:
                    v = v.bitcast(dt)
                v = v[:, 0:nel]
                if len(shape) == 3:
                    v = v.rearrange("p (a b) -> p a b", a=shape[1])
                elif len(shape) == 4:
                    v = v.rearrange("p (a b c) -> p a b c", a=shape[1], b=shape[2])
                self.n += 1
                return v, ("ar", self.n)

        A = Arena()

        p.op("pool", lambda e: e.memset(ident[:], 0.0), writes=["ident"])
        p.op("pool", lambda e: e.affine_select(out=ident[:], in_=ident[:], pattern=[[1, 128]],
                                               compare_op=ALU.not_equal, fill=1.0, base=0,
                                               channel_multiplier=-1),
             reads=["ident"], writes=["ident"])
        p.op("pool", lambda e: e.memset(ones_bf[:], 1.0 / D), writes=["ones_bf"])
        p.op("pool", lambda e: e.memset(ones_f[:], 1.0), writes=["ones_f"])
        p.op("pool", lambda e: e.memset(EPS_AP[:], EPS), writes=["eps"])
        for g in range(5):
            p.op("sp", lambda e, g=g: e.dma_start(out=gains[:, g, :],
                                                  in_=gains_d[g].rearrange("(c p) -> p c", p=128),
                                                  allow_slow_non_contiguous=True),
                 writes=["gains"])

        def load_x(s):
            A.reset()
            xio = [A.alloc((128, D), F32) for i in range(2)]
            for tl in range(S // 128):
                xb, key = xio[tl % 2]
                t0 = s * S + tl * 128
                p.op("sp", lambda e, xb=xb, t0=t0: e.dma_start(out=xb, in_=x[t0:t0 + 128, :]), writes=[key])
                for half in range(2):
                    b = p.nb()
                    for j in range(4):
                        c = half * 4 + j
                        p.op("pe", lambda e, b=b, j=j, c=c, xb=xb: e.transpose(
                            out=ps[:, b, j * 128:(j + 1) * 128], in_=xb[:, c * 128:(c + 1) * 128], identity=ident[:]),
                            reads=[key, "ident"], writes=[("ps", b)])
                    dst = h[:, half * 4:half * 4 + 4, tl * 128:(tl + 1) * 128]
                    src = ps[:, b, :].rearrange("p (j t) -> p j t", j=4)
                    if half == 0:
                        p.op("act", lambda e, dst=dst, src=src: e.activation(out=dst, in_=src, func=AF.Copy),
                             reads=[("ps", b)], writes=[("h", tl // 4)])
                    else:
                        p.op("dve", lambda e, dst=dst, src=src: e.tensor_copy(out=dst, in_=src),
                             reads=[("ps", b)], writes=[("h", tl // 4)])

        def store_y(s, src_t):
            A.reset()
            xio = [A.alloc((128, D), F32) for i in range(2)]
            outs = []
            for tl in range(S // 128):
                xb, key = xio[tl % 2]
                t0 = s * S + tl * 128
                for half in range(2):
                    b = p.nb()
                    for j in range(4):
                        c = half * 4 + j
                        p.op("pe", lambda e, b=b, j=j, c=c, tl=tl: e.transpose(
                            out=ps[:, b, j * 128:(j + 1) * 128], in_=src_t[:, c, tl * 128:(tl + 1) * 128], identity=ident[:]),
                            reads=[("h", tl // 4), "ident"], writes=[("ps", b)])
                    dst = xb[:, half * 512:(half + 1) * 512]
                    if half == 0:
                        p.op("act", lambda e, dst=dst, b=b: e.activation(out=dst, in_=ps[:, b, :], func=AF.Copy),
                             reads=[("ps", b)], writes=[key])
                    else:
                        p.op("dve", lambda e, dst=dst, b=b: e.tensor_copy(out=dst, in_=ps[:, b, :]),
                             reads=[("ps", b)], writes=[key])
                outs.append(p.op("sp", lambda e, xb=xb, t0=t0: e.dma_start(out=y[t0:t0 + 128, :], in_=xb), reads=[key]))
            return outs

        def rms_norm(gi, out_t, out_key, in_place=False):
            A.reset()
            sq, sqk = A.alloc((128, 8, 512), BF16)
            rs = [A.alloc((128, 512), F32) for i in range(2)]
            for tt in range(4):
                sl = slice(tt * 512, (tt + 1) * 512)
                p.op("act", lambda e, sl=sl: e.activation(out=sq, in_=h[:, :, sl], func=AF.Square),
                     reads=[("h", tt)], writes=[sqk])
                b = p.nb()
                for c in range(8):
                    p.op("pe", lambda e, b=b, c=c: e.matmul(ps[:, b, :], ones_bf[:], sq[:, c, :], start=(c == 0), stop=(c == 7)),
                         reads=[sqk, "ones_bf"], writes=[("ps", b)])
                r, rk = rs[tt % 2]
                p.op("act", lambda e, r=r, b=b: e.activation(out=r, in_=ps[:, b, :], func=AF.Sqrt, bias=EPS_AP[:], scale=1.0),
                     reads=[("ps", b), "eps"], writes=[rk])
                p.op("dve", lambda e, r=r: e.reciprocal(out=r, in_=r), reads=[rk], writes=[rk])
                for c in range(8):
                    wr = [("h", tt)] if in_place else [(out_key, tt)]
                    p.op("dve", lambda e, c=c, sl=sl, r=r: e.scalar_tensor_tensor(
                        out=out_t[:, c, sl], in0=h[:, c, sl], scalar=gains[:, gi, c:c + 1], in1=r,
                        op0=ALU.mult, op1=ALU.mult),
                        reads=[("h", tt), rk, "gains"], writes=wr)

        def ffn(layer, gi):
            rms_norm(gi, hn, "hn")
            A.reset()
            NGM = max(c1 - c0 for c0, c1 in FFN_GROUPS)
            win = [A.alloc((128, 8, 2, NGM * 128), BF16) for i in range(2)]
            wout = [A.alloc((128, NGM, D), BF16) for i in range(2)]
            actb = [A.alloc((128, NGM, 512), BF16) for i in range(2)]
            sg = [A.alloc((128, 512), F32) for i in range(2)]
            wi_v = w_fi[layer].rearrange("(kc p) n -> p kc n", p=128)
            wo_v = w_fo[layer].rearrange("(hc p) n -> p hc n", p=128)
            nsg = 0
            nact = 0
            for gidx, (c0, c1) in enumerate(FFN_GROUPS):
                ng = c1 - c0
                wi, wik = win[gidx % 2]
                wo, wok = wout[gidx % 2]
                for gu in range(2):
                    for kc in range(8):
                        p.op("pq", lambda e, wi=wi, gu=gu, kc=kc, c0=c0, ng=ng: e.dma_start(
                            out=wi[:, kc, gu, 0:ng * 128],
                            in_=wi_v[:, kc, gu * FH + c0 * 128: gu * FH + c0 * 128 + ng * 128]),
                            writes=[wik])
                for j in range(ng):
                    p.op("pq", lambda e, wo=wo, j=j, c0=c0: e.dma_start(out=wo[:, j, :], in_=wo_v[:, c0 + j, :]),
                         writes=[wok])
                for tt in range(4):
                    sl = slice(tt * 512, (tt + 1) * 512)
                    ab, akey = actb[nact % 2]
                    nact += 1
                    for j in range(ng):
                        bg = p.nb()
                        bu = p.nb()
                        for gu, bb in ((0, bg), (1, bu)):
                            for kc in range(8):
                                p.op("pe", lambda e, bb=bb, gu=gu, kc=kc, j=j, wi=wi, sl=sl: e.matmul(
                                    ps[:, bb, :], wi[:, kc, gu, j * 128:(j + 1) * 128], hn[:, kc, sl],
                                    start=(kc == 0), stop=(kc == 7)),
                                    reads=[wik, ("hn", tt)], writes=[("ps", bb)])
                        sgb, skey = sg[nsg % 2]
                        nsg += 1
                        p.op("act", lambda e, sgb=sgb, bg=bg: e.activation(out=sgb, in_=ps[:, bg, :], func=AF.Silu),
                             reads=[("ps", bg)], writes=[skey])
                        p.op("dve", lambda e, ab=ab, j=j, sgb=sgb, bu=bu: e.tensor_tensor(
                            out=ab[:, j, :], in0=sgb, in1=ps[:, bu, :], op=ALU.mult),
                            reads=[skey, ("ps", bu)], writes=[akey])
                    for oc in range(8):
                        bo = p.nb()
                        for j in range(ng):
                            p.op("pe", lambda e, bo=bo, j=j, oc=oc, wo=wo, ab=ab, ng=ng: e.matmul(
                                ps[:, bo, :], wo[:, j, oc * 128:(oc + 1) * 128], ab[:, j, :],
                                start=(j == 0), stop=(j == ng - 1)),
                                reads=[wok, akey], writes=[("ps", bo)])
                        p.op("dve", lambda e, bo=bo, oc=oc, sl=sl: e.tensor_tensor(
                            out=h[:, oc, sl], in0=h[:, oc, sl], in1=ps[:, bo, :], op=ALU.add),
                            reads=[("ps", bo), ("h", tt)], writes=[("h", tt)])

        def attn():
            rms_norm(0, hn, "hn")
            A.reset()
            slopes = alibi_slopes()
            qT, qk_ = A.alloc((128, S), BF16)
            kT, kk_ = A.alloc((128, 4096), BF16)
            vaug, vk_ = A.alloc((128, 32, 256), BF16)
            oacc = [A.alloc((128, S), F32) for i in range(2)]
            oT, otk = A.alloc((128, 2, S), BF16)
            wqkv = [A.alloc((128, 8, 3, 128), BF16) for i in range(2)]
            wo_b = [A.alloc((128, 2, D), BF16) for i in range(2)]
            rmb = {k: A.alloc((128, 2, 2, 128), F32) for k in ("FI", "II", "IL", "BB")}
            tS = [A.alloc((128, 512), F32) for i in range(3)]
            pT = [A.alloc((128, 512), BF16) for i in range(3)]
            rec, reck = A.alloc((128, 512), F32)
            ri, rik = A.alloc((128, 2, 128), F32)
            mk, mkk = A.alloc((128, 2, 128), F32)

            p.op("pool", lambda e: e.iota(out=ri, pattern=[[128, 2], [-1, 128]], base=-64, channel_multiplier=1,
                                          allow_small_or_imprecise_dtypes=True), writes=[rik])
            p.op("act", lambda e: e.activation(out=ri, in_=ri, func=AF.Abs), reads=[rik], writes=[rik])
            p.op("dve", lambda e: e.tensor_scalar(out=mk, in0=ri, scalar1=-64.0, scalar2=0.0, op0=ALU.add, op1=ALU.max),
                 reads=[rik], writes=[mkk])
            p.op("dve", lambda e: e.tensor_scalar(out=mk, in0=mk, scalar1=BIG, scalar2=None, op0=ALU.mult),
                 reads=[mkk], writes=[mkk])
            p.op("dve", lambda e: e.tensor_tensor(out=ri, in0=ri, in1=mk, op=ALU.add), reads=[rik, mkk], writes=[rik])
            for k_, (t_, tk_) in rmb.items():
                for qi in range(2):
                    p.op("pool", lambda e, t_=t_, qi=qi: e.tensor_copy(out=t_[:, qi], in_=ri), reads=[rik], writes=[tk_])
            p.op("pool", lambda e: e.memset(rmb["FI"][0][0:64, 0, 0, :], BIG), writes=[rmb["FI"][1]])
            p.op("pool", lambda e: e.memset(rmb["IL"][0][64:128, 1, 1, :], BIG), writes=[rmb["IL"][1]])
            for qi in range(2):
                p.op("pool", lambda e, qi=qi: e.memset(rmb["BB"][0][0:64, qi, 0, :], BIG), writes=[rmb["BB"][1]])
                p.op("pool", lambda e, qi=qi: e.memset(rmb["BB"][0][64:128, qi, 1, :], BIG), writes=[rmb["BB"][1]])
            p.op("pool", lambda e: e.memset(vaug, 0.0), writes=[vk_])
            p.op("pool", lambda e: e.memset(vaug[:, :, 64:65], 1.0), writes=[vk_])
            p.op("pool", lambda e: e.memset(vaug[:, :, 128:129], 1.0), writes=[vk_])

            wq_v = w_qkv.rearrange("(kc p) n -> p kc n", p=128)
            wao_v = w_ao.rearrange("(hp p) n -> p hp n", p=128)
            it = 0
            for rnd in range(4):
                wo, wok = wo_b[rnd % 2]
                p.op("pq", lambda e, wo=wo, rnd=rnd: e.dma_start(out=wo, in_=wao_v[:, rnd * 2:rnd * 2 + 2, :]), writes=[wok])
                for hpi in range(2):
                    hp = rnd * 2 + hpi
                    for g, (window, d) in enumerate(GROUPS):
                        L = S // d
                        LB = L + 128
                        nq = L // 128
                        ntile = nq + 1
                        wq, wqk = wqkv[it % 2]
                        it += 1
                        for which in range(3):
                            col0 = g * 3072 + which * 1024 + hp * 128
                            p.op("pq", lambda e, wq=wq, which=which, col0=col0: e.dma_start(
                                out=wq[:, :, which, :], in_=wq_v[:, :, col0:col0 + 128]), writes=[wqk])
                        kTv = kT[:, 0:d * LB].rearrange("p (r m) -> p r m", r=d)
                        p.op("pool", lambda e, kTv=kTv: e.memset(kTv[:, :, 0:64], 0.0), writes=[kk_])
                        p.op("pool", lambda e, kTv=kTv, L=L: e.memset(kTv[:, :, 64 + L:128 + L], 0.0), writes=[kk_])
                        qTv = qT.rearrange("p (r l) -> p r l", r=d)
                        for which in range(2):
                            for tt in range(4):
                                b = p.nb()
                                for kc in range(8):
                                    p.op("pe", lambda e, b=b, kc=kc, wq=wq, which=which, tt=tt: e.matmul(
                                        ps[:, b, :], wq[:, kc, which, :], hn[:, kc, tt * 512:(tt + 1) * 512],
                                        start=(kc == 0), stop=(kc == 7)),
                                        reads=[wqk, ("hn", tt)], writes=[("ps", b)])
                                src = ps[:, b, :].rearrange("p (l r) -> p r l", r=d)
                                n_l = 512 // d
                                if which == 0:
                                    dst = qTv[:, :, tt * n_l:(tt + 1) * n_l]
                                    wkey = qk_
                                else:
                                    dst = kTv[:, :, 64 + tt * n_l:64 + (tt + 1) * n_l]
                                    wkey = kk_
                                p.op("act", lambda e, dst=dst, src=src: e.activation(out=dst, in_=src, func=AF.Copy),
                                     reads=[("ps", b)], writes=[wkey])
                        slot = 0
                        b = None
                        for r in range(d):
                            for m in range(ntile):
                                pos0 = max(0, 128 * m - 64)
                                pos1 = min(L, 128 * m + 64)
                                nv = pos1 - pos0
                                po = 64 if m == 0 else 0
                                if slot % 4 == 0:
                                    b = p.nb()
                                t_start = pos0 * d + r
                                t_stop = t_start + d * (nv - 1) + 1
                                for kc in range(8):
                                    p.op("pe", lambda e, b=b, kc=kc, wq=wq, po=po, nv=nv, t_start=t_start, t_stop=t_stop, d=d, sl_=slot % 4: e.matmul(
                                        ps[po:po + nv, b, sl_ * 128:(sl_ + 1) * 128], hn[:, kc, t_start:t_stop:d], wq[:, kc, 2, :],
                                        start=(kc == 0), stop=(kc == 7)),
                                        reads=[wqk] + [("hn", t_) for t_ in range(4)], writes=[("ps", b)])
                                tile_i = r * ntile + m
                                dst = vaug[po:po + nv, tile_i, :].rearrange("p (a b) -> p a b", a=4)[:, 0:4:3, :]
                                src = ps[po:po + nv, b, (slot % 4) * 128:(slot % 4 + 1) * 128].rearrange("p (a b) -> p a b", a=2)
                                p.op("act", lambda e, dst=dst, src=src: e.activation(out=dst, in_=src, func=AF.Copy),
                                     reads=[("ps", b)], writes=[vk_])
                                slot += 1
                        units = []
                        for e_ in range(2):
                            for quad in range(4):
                                for bp in range(2):
                                    units.append((e_, quad, bp))
                        pend = []
                        obank = {}

                        def ttype(qt):
                            if nq == 1:
                                return "B"
                            return "F" if qt == 0 else ("L" if qt == nq - 1 else "I")

                        def emit_pv(u, info):
                            e_, quad, bp = u
                            pt, ptk = info
                            if bp == 0:
                                obank[(e_, quad)] = p.nb()
                            bo = obank[(e_, quad)]
                            M = 65 if e_ == 0 else 128
                            c0 = 0 if e_ == 0 else 128
                            for qi in range(2):
                                n = quad * 4 + bp * 2 + qi
                                r, qt = divmod(n, nq)
                                for j in range(2):
                                    tile_i = r * ntile + qt + j
                                    p.op("pe", lambda e, bo=bo, M=M, c0=c0, tile_i=tile_i, pt=pt, qi=qi, j=j, bp=bp: e.matmul(
                                        ps[0:M, bo, (2 * bp + qi) * 128:(2 * bp + qi + 1) * 128],
                                        vaug[:, tile_i, c0:c0 + M], pt[:, (qi * 2 + j) * 128:(qi * 2 + j + 1) * 128],
                                        start=(j == 0), stop=(j == 1)),
                                        reads=[vk_, ptk], writes=[("ps", bo)])
                            if bp == 1:
                                oa, oak = oacc[e_]
                                if d == 1:
                                    dst = oa[0:M, quad * 512:(quad + 1) * 512]
                                    src = ps[0:M, bo, :]
                                elif d == 4:
                                    dst = oa[0:M, quad:S:4]
                                    src = ps[0:M, bo, :]
                                else:
                                    dst = oa[0:M, :].rearrange("p (l r) -> p r l", r=16)[:, quad * 4:quad * 4 + 4, :]
                                    src = ps[0:M, bo, :].rearrange("p (r l) -> p r l", r=4)
                                if g == 0:
                                    p.op("act", lambda e, dst=dst, src=src: e.activation(out=dst, in_=src, func=AF.Copy),
                                         reads=[("ps", bo)], writes=[oak])
                                else:
                                    p.op("dve", lambda e, dst=dst, src=src: e.tensor_tensor(out=dst, in0=src, in1=dst, op=ALU.add),
                                         reads=[("ps", bo), oak], writes=[oak])

                        for ui, u in enumerate(units):
                            e_, quad, bp = u
                            hd = 2 * hp + e_
                            cc = slopes[hd] * d
                            bS = p.nb()
                            types = ""
                            for qi in range(2):
                                n = quad * 4 + bp * 2 + qi
                                r, qt = divmod(n, nq)
                                types += ttype(qt)
                                for j in range(2):
                                    m = qt + j
                                    p.op("pe", lambda e, bS=bS, e_=e_, r=r, m=m, n=n, qi=qi, j=j, LB=LB: e.matmul(
                                        ps[:, bS, (qi * 2 + j) * 128:(qi * 2 + j + 1) * 128],
                                        kT[64 * e_:64 * e_ + 64, r * LB + 128 * m:r * LB + 128 * m + 128],
                                        qT[64 * e_:64 * e_ + 64, n * 128:(n + 1) * 128],
                                        start=True, stop=True),
                                        reads=[kk_, qk_], writes=[("ps", bS)])
                            rm, rmk = rmb[{"FI": "FI", "II": "II", "IL": "IL", "BB": "BB"}[types]]
                            ts_, tsk = tS[ui % 3]
                            pt, ptk = pT[ui % 3]
                            p.op("dve", lambda e, ts_=ts_, rm=rm, cc=cc, bS=bS: e.scalar_tensor_tensor(
                                out=ts_, in0=rm.rearrange("p a b c -> p (a b c)"), scalar=-8.0 * cc, in1=ps[:, bS, :],
                                op0=ALU.mult, op1=ALU.add),
                                reads=[rmk, ("ps", bS)], writes=[tsk])
                            p.op("act", lambda e, pt=pt, ts_=ts_: e.activation(out=pt, in_=ts_, func=AF.Exp, scale=0.125),
                                 reads=[tsk], writes=[ptk])
                            pend.append((u, (pt, ptk)))
                            if len(pend) > 2:
                                emit_pv(*pend.pop(0))
                        while pend:
                            emit_pv(*pend.pop(0))
                    for e_ in range(2):
                        oa, oak = oacc[e_]
                        row = 64 if e_ == 0 else 0
                        for tt in range(4):
                            sl = slice(tt * 512, (tt + 1) * 512)
                            b = p.nb()
                            p.op("pe", lambda e, b=b, row=row, oa=oa, sl=sl: e.matmul(
                                ps[:, b, :], ones_f[row:row + 1, :], oa[row:row + 1, sl], start=True, stop=True),
                                reads=[oak, "ones_f"], writes=[("ps", b)])
                            pr = slice(64 * e_, 64 * e_ + 64)
                            p.op("dve", lambda e, b=b, pr=pr: e.reciprocal(out=rec[pr, :], in_=ps[pr, b, :]),
                                 reads=[("ps", b)], writes=[reck])
                            p.op("dve", lambda e, pr=pr, oa=oa, sl=sl, hpi=hpi: e.tensor_tensor(
                                out=oT[pr, hpi, sl], in0=oa[pr, sl], in1=rec[pr, :], op=ALU.mult),
                                reads=[reck, oak], writes=[otk])
                for tt in range(4):
                    sl = slice(tt * 512, (tt + 1) * 512)
                    for oc in range(8):
                        bo = p.nb()
                        for hpi in range(2):
                            p.op("pe", lambda e, bo=bo, hpi=hpi, oc=oc, wo=wo, sl=sl: e.matmul(
                                ps[:, bo, :], wo[:, hpi, oc * 128:(oc + 1) * 128], oT[:, hpi, sl],
                                start=(hpi == 0), stop=(hpi == 1)),
                                reads=[wok, otk], writes=[("ps", bo)])
                        p.op("dve", lambda e, bo=bo, oc=oc, sl=sl: e.tensor_tensor(
                            out=h[:, oc, sl], in0=h[:, oc, sl], in1=ps[:, bo, :], op=ALU.add),
                            reads=[("ps", bo), ("h", tt)], writes=[("h", tt)])


        def bcast(ap2, n):
            return bass.AP(ap2.tensor, ap2.offset, [list(ap2.ap[0]), list(ap2.ap[1]), [0, n]])

        def bcast_mid(ap2, n):
            return bass.AP(ap2.tensor, ap2.offset, [list(ap2.ap[0]), [0, n], list(ap2.ap[1])])

        def s5():
            rms_norm(2, hn, "hn")
            A.reset()
            uT, utk = A.alloc((128, 8, S), BF16)
            btmp = [A.alloc((128, 64, 16), F32) for _ in range(2)]
            bcl = [A.alloc((128, 8, 128), BF16) for _ in range(2)]
            cpl = [A.alloc((128, 8, 128), BF16) for _ in range(2)]
            TA = [A.alloc((128, 16, 64), F32) for _ in range(2)]
            TB = [A.alloc((128, 16, 64), F32) for _ in range(2)]
            pw = [A.alloc((128, 11, 64), F32) for _ in range(3)]
            colmask, cmk = A.alloc((128, 8, 128), BF16)
            rowmask, rmk_ = A.alloc((128, 8), F32)
            blockmask, bmk = A.alloc((128, 128), BF16)
            dsk, dskk = A.alloc((128, 8), F32)
            mark = A.off
            def T64():
                return A.alloc((128, 64), F32)
            are, aim, dtt, th, mag, sn, cs, t1, t2, fr, fi, inv = [T64() for _ in range(12)]
            bpl = [A.alloc((128, 64, 16), F32) for _ in range(2)]
            cnat, cnk = A.alloc((128, 8, 2, 64), F32)
            anat, ank = A.alloc((128, 2, 64), F32)
            e01, e01k = A.alloc((128, 2, 128), F32)
            ldt, ldk = A.alloc((128, 2, 64), F32)
            wsi = [A.alloc((128, 8, 128), BF16)] * 2

            def tt_(eng, o, a, b_, op):
                p.op(eng, lambda e: e.tensor_tensor(out=o[0], in0=a[0], in1=b_[0], op=op), reads=[a[1], b_[1]], writes=[o[1]])

            def ts_(eng, o, a, s1, s2, op0, op1=None):
                if op1 is None:
                    p.op(eng, lambda e: e.tensor_scalar(out=o[0], in0=a[0], scalar1=s1, scalar2=None, op0=op0), reads=[a[1]], writes=[o[1]])
                else:
                    p.op(eng, lambda e: e.tensor_scalar(out=o[0], in0=a[0], scalar1=s1, scalar2=s2, op0=op0, op1=op1), reads=[a[1]], writes=[o[1]])

            def act_(o, a, func, scale=1.0):
                p.op("act", lambda e: e.activation(out=o[0], in_=a[0], func=func, scale=scale), reads=[a[1]], writes=[o[1]])

            p.op("pool", lambda e: e.memset(colmask, 0.0), writes=[cmk])
            for g8 in range(8):
                p.op("pool", lambda e, g8=g8: e.memset(colmask[:, g8, g8 * 16:(g8 + 1) * 16], 1.0), writes=[cmk])
            p.op("pool", lambda e: e.memset(rowmask, 1.0), writes=[rmk_])
            p.op("pool", lambda e: e.affine_select(out=rowmask, in_=rowmask, pattern=[[-16, 8]], compare_op=ALU.is_ge,
                                                   fill=0.0, base=0, channel_multiplier=1), reads=[rmk_], writes=[rmk_])
            p.op("pool", lambda e: e.affine_select(out=rowmask, in_=rowmask, pattern=[[16, 8]], compare_op=ALU.is_ge,
                                                   fill=0.0, base=15, channel_multiplier=-1), reads=[rmk_], writes=[rmk_])
            p.op("pool", lambda e: e.memset(e01, 0.0), writes=[e01k])
            p.op("pool", lambda e: e.memset(e01[0:1, 0, 0:64], 1.0), writes=[e01k])
            p.op("pool", lambda e: e.memset(e01[0:1, 1, 64:128], 1.0), writes=[e01k])
            p.op("sp", lambda e: e.dma_start(out=dsk, in_=d_skip_d.rearrange("(c p) -> p c", p=128), allow_slow_non_contiguous=True), writes=[dskk])
            for d_ in range(2):
                p.op("sp", lambda e, d_=d_: e.dma_start(out=ldt[0:1, d_, :], in_=log_dt_d[d_:d_ + 1, :]), writes=[ldk])
            b0 = p.nb()
            for d_ in range(2):
                p.op("pe", lambda e, d_=d_, b0=b0: e.matmul(ps[:, b0, 0:64], e01[0:1, d_, :], ldt[0:1, d_, :], start=(d_ == 0), stop=(d_ == 1)),
                     reads=[e01k, ldk], writes=[("ps", b0)])
            p.op("act", lambda e, b0=b0: e.activation(out=dtt[0], in_=ps[:, b0, 0:64], func=AF.Exp), reads=[("ps", b0)], writes=[dtt[1]])
            for src_d, dst in ((a_re_d, are), (a_im_d, aim)):
                p.op("sp", lambda e, src_d=src_d: e.dma_start(out=anat[0:64], in_=src_d.rearrange("d g p -> g d p")), writes=[ank])
                b0 = p.nb()
                p.op("pe", lambda e, b0=b0: e.transpose(out=ps[:, b0, 0:64], in_=anat[0:64].rearrange("g d p -> g (d p)"), identity=ident[0:64, 0:64]),
                     reads=[ank, "ident"], writes=[("ps", b0)])
                p.op("act", lambda e, b0=b0, dst=dst: e.activation(out=dst[0], in_=ps[:, b0, 0:64], func=AF.Copy), reads=[("ps", b0)], writes=[dst[1]])
            tt_("dve", t1, are, dtt, ALU.mult)
            act_(mag, t1, AF.Exp)
            tt_("dve", th, aim, dtt, ALU.mult)
            act_(sn, th, AF.Sin, 1.0 / 16.0)
            tt_("dve", cs, sn, sn, ALU.mult)
            ts_("dve", cs, cs, -2.0, 1.0, ALU.mult, ALU.add)
            act_(sn, th, AF.Sin, 1.0 / 8.0)
            for _ in range(3):
                tt_("dve", t1, sn, cs, ALU.mult)
                tt_("dve", t2, sn, sn, ALU.mult)
                ts_("dve", sn, t1, 2.0, None, ALU.mult)
                ts_("dve", cs, t2, -2.0, 1.0, ALU.mult, ALU.add)
            lr = (pw[0][0][:, 0, :], pw[0][1])
            li = (pw[1][0][:, 0, :], pw[1][1])
            tt_("dve", lr, mag, cs, ALU.mult)
            tt_("dve", li, mag, sn, ALU.mult)
            tt_("dve", t1, are, are, ALU.mult)
            tt_("dve", t2, aim, aim, ALU.mult)
            tt_("dve", inv, t1, t2, ALU.add)
            p.op("dve", lambda e: e.reciprocal(out=inv[0], in_=inv[0]), reads=[inv[1]], writes=[inv[1]])
            ts_("dve", mag, lr, -1.0, None, ALU.add)
            tt_("dve", t1, mag, are, ALU.mult)
            tt_("dve", t2, li, aim, ALU.mult)
            tt_("dve", t1, t1, t2, ALU.add)
            tt_("dve", fr, t1, inv, ALU.mult)
            tt_("dve", t1, li, are, ALU.mult)
            tt_("dve", t2, mag, aim, ALU.mult)
            tt_("dve", t1, t1, t2, ALU.subtract)
            tt_("dve", fi, t1, inv, ALU.mult)
            for k in range(1, 11):
                pr_ = (pw[0][0][:, k - 1, :], pw[0][1]); pi_ = (pw[1][0][:, k - 1, :], pw[1][1])
                nr_ = (pw[0][0][:, k, :], pw[0][1]); ni_ = (pw[1][0][:, k, :], pw[1][1])
                tt_("dve", t1, pr_, pr_, ALU.mult)
                tt_("dve", t2, pi_, pi_, ALU.mult)
                tt_("dve", nr_, t1, t2, ALU.subtract)
                tt_("dve", t1, pr_, pi_, ALU.mult)
                ts_("dve", ni_, t1, 2.0, None, ALU.mult)
            ts_("dve", pw[2], pw[1], -1.0, None, ALU.mult)
            for ri, src_d in enumerate((b_re_d, b_im_d)):
                for d_ in range(2):
                    p.op("sp", lambda e, ri=ri, d_=d_, src_d=src_d: e.dma_start(
                        out=bpl[ri][0][64 * d_:64 * d_ + 64], in_=src_d[d_].rearrange("g p c -> p g c")), writes=[bpl[ri][1]])
            def cmul(o, a, f_):
                for c_ in range(16):
                    p.op("dve", lambda e, o=o, a=a, f_=f_, c_=c_: e.tensor_tensor(out=o[0][:, :, c_], in0=a[0][:, :, c_], in1=f_[0], op=ALU.mult),
                         reads=[a[1], f_[1]], writes=[o[1]])
            cmul(btmp[0], bpl[0], fr)
            cmul(btmp[1], bpl[1], fi)
            tt_("dve", btmp[0], btmp[0], btmp[1], ALU.subtract)
            cmul(btmp[1], bpl[1], fr)
            cmul(bpl[1], bpl[0], fi)
            tt_("dve", btmp[1], btmp[1], bpl[1], ALU.add)
            for ri in range(2):
                for half in range(2):
                    b0 = p.nb()
                    for jj in range(4):
                        j = half * 4 + jj
                        p.op("pe", lambda e, b0=b0, jj=jj, j=j, ri=ri: e.transpose(
                            out=ps[:, b0, jj * 128:(jj + 1) * 128],
                            in_=btmp[ri][0][:, j * 8:(j + 1) * 8, :].rearrange("p g c -> p (g c)"), identity=ident[:]),
                            reads=[btmp[ri][1], "ident"], writes=[("ps", b0)])
                    p.op("act", lambda e, b0=b0, half=half, ri=ri: e.activation(
                        out=bcl[ri][0][:, half * 4:half * 4 + 4, :], in_=ps[:, b0, :].rearrange("p (j q) -> p j q", j=4), func=AF.Copy),
                        reads=[("ps", b0)], writes=[bcl[ri][1]])
            for ri, src_d in enumerate((c_re_d, c_im_d)):
                for d_ in range(2):
                    p.op("sp", lambda e, d_=d_, src_d=src_d: e.dma_start(
                        out=cnat[:, :, d_, :], in_=src_d[d_].rearrange("(j g) c p -> (g c) j p", g=8)), writes=[cnk])
                for half in range(2):
                    b0 = p.nb()
                    for jj in range(4):
                        j = half * 4 + jj
                        p.op("pe", lambda e, b0=b0, jj=jj, j=j: e.transpose(
                            out=ps[:, b0, jj * 128:(jj + 1) * 128], in_=cnat[:, j, :, :].rearrange("p d q -> p (d q)"), identity=ident[:]),
                            reads=[cnk, "ident"], writes=[("ps", b0)])
                    p.op("act", lambda e, b0=b0, half=half, ri=ri: e.activation(
                        out=cpl[ri][0][:, half * 4:half * 4 + 4, :], in_=ps[:, b0, :].rearrange("p (j q) -> p j q", j=4),
                        func=AF.Copy, scale=(1.0 if ri == 0 else -1.0)),
                        reads=[("ps", b0)], writes=[cpl[ri][1]])

            p.op("pool", lambda e: e.memset(blockmask, 0.0), writes=[bmk])
            for g8 in range(8):
                p.op("dve", lambda e, g8=g8: e.scalar_tensor_tensor(out=blockmask, in0=colmask[:, g8, :], scalar=rowmask[:, g8:g8 + 1],
                                                                     in1=blockmask, op0=ALU.mult, op1=ALU.add),
                     reads=[cmk, rmk_, bmk], writes=[bmk])
            lam_r = pw[0][0][:, 0, :]
            lam_i = pw[1][0][:, 0, :]

            def tstep(Tt, dst_i, src_i, P_):
                sr = Tt[0][0][P_, src_i, :]; si = Tt[1][0][P_, src_i, :]
                dr = Tt[0][0][P_, dst_i, :]; di = Tt[1][0][P_, dst_i, :]
                rd = [Tt[0][1], Tt[1][1], pw[0][1], pw[1][1]]
                a_ = t1[0][P_]; b_ = t2[0][P_]
                p.op("dve", lambda e: e.tensor_tensor(out=a_, in0=sr, in1=lam_r[P_], op=ALU.mult), reads=rd, writes=[t1[1]])
                p.op("dve", lambda e: e.tensor_tensor(out=b_, in0=si, in1=lam_i[P_], op=ALU.mult), reads=rd, writes=[t2[1]])
                p.op("dve", lambda e: e.tensor_tensor(out=dr, in0=a_, in1=b_, op=ALU.subtract), reads=[t1[1], t2[1]], writes=[Tt[0][1]])
                p.op("dve", lambda e: e.tensor_tensor(out=a_, in0=sr, in1=lam_i[P_], op=ALU.mult), reads=rd, writes=[t1[1]])
                p.op("dve", lambda e: e.tensor_tensor(out=b_, in0=si, in1=lam_r[P_], op=ALU.mult), reads=rd, writes=[t2[1]])
                p.op("dve", lambda e: e.tensor_tensor(out=di, in0=a_, in1=b_, op=ALU.add), reads=[t1[1], t2[1]], writes=[Tt[1][1]])

            PF = slice(0, 64); PB = slice(64, 128)
            p.op("pool", lambda e: e.memset(TA[0][0][PF, 15, :], 1.0), writes=[TA[0][1]])
            p.op("pool", lambda e: e.memset(TA[1][0][PF, 15, :], 0.0), writes=[TA[1][1]])
            p.op("pool", lambda e: e.memset(TA[0][0][PB, 0, :], 1.0), writes=[TA[0][1]])
            p.op("pool", lambda e: e.memset(TA[1][0][PB, 0, :], 0.0), writes=[TA[1][1]])
            for i in range(15, 0, -1):
                tstep(TA, i - 1, i, PF)
            for i in range(0, 15):
                tstep(TA, i + 1, i, PB)
            for ri in range(2):
                p.op("dve", lambda e, ri=ri: e.tensor_copy(out=TB[ri][0][PF, 0, :], in_=pw[ri][0][PF, 0, :]), reads=[pw[ri][1]], writes=[TB[ri][1]])
                p.op("dve", lambda e, ri=ri: e.tensor_copy(out=TB[ri][0][PB, 15, :], in_=pw[ri][0][PB, 0, :]), reads=[pw[ri][1]], writes=[TB[ri][1]])
            for i in range(0, 15):
                tstep(TB, i + 1, i, PF)
            for i in range(15, 0, -1):
                tstep(TB, i - 1, i, PB)
            wsi_v = w_si_d.rearrange("(kc p) n -> p kc n", p=128)
            for j in range(8):
                ws, wsk = wsi[j % 2]
                p.op("pq", lambda e, ws=ws, j=j: e.dma_start(out=ws, in_=wsi_v[:, :, j * 128:(j + 1) * 128]), writes=[wsk])
                for tt in range(4):
                    b0 = p.nb()
                    for kc in range(8):
                        p.op("pe", lambda e, b0=b0, kc=kc, ws=ws, tt=tt: e.matmul(
                            ps[:, b0, :], ws[:, kc, :], hn[:, kc, tt * 512:(tt + 1) * 512], start=(kc == 0), stop=(kc == 7)),
                            reads=[wsk, ("hn", tt)], writes=[("ps", b0)])
                    p.op("act", lambda e, b0=b0, j=j, tt=tt: e.activation(out=uT[:, j, tt * 512:(tt + 1) * 512], in_=ps[:, b0, :], func=AF.Copy),
                         reads=[("ps", b0)], writes=[utk])

            A.off = mark
            p.barrier()
            zb = [A.alloc((128, 1024), F32) for _ in range(2)]
            vpp = [[A.alloc((128, 256), F32) for _ in range(2)] for _ in range(2)]
            accA = A.alloc((128, 128), F32)
            accB = A.alloc((128, 128), F32)
            sx, sxk = A.alloc((128, 8, 2, 128), BF16)
            wpad = [A.alloc((128, 8, 128), BF16) for _ in range(2)]
            ktile = [A.alloc((128, 128), BF16) for _ in range(4)]
            bplj = [A.alloc((128, 128), BF16) for _ in range(2)]
            w1 = A.alloc((128, 128), F32)
            w2 = A.alloc((128, 128), F32)
            wri = [A.alloc((128, 128), BF16) for _ in range(2)]
            ddiag, ddk = A.alloc((128, 128), BF16)
            ug_ap, ugk = A.alloc((128, S), BF16)
            gx = [None, A.alloc((128, 512), F32), A.alloc((128, 512), F32)]
            mark2 = A.off
            for pp_ in range(2):
                for ri in range(2):
                    p.op("pool", lambda e, pp_=pp_, ri=ri: e.memset(vpp[pp_][ri][0], 0.0), writes=[vpp[pp_][ri][1]])
            p.op("pool", lambda e: e.memset(sx, 0.0), writes=[sxk])

            def make_w(Tt, i, j):
                tr = bcast(Tt[0][0][:, i, j * 8:(j + 1) * 8], 16)
                ti = bcast(Tt[1][0][:, i, j * 8:(j + 1) * 8], 16)
                c0 = cpl[0][0][:, j, :].rearrange("p (g c) -> p g c", c=16)
                c1 = cpl[1][0][:, j, :].rearrange("p (g c) -> p g c", c=16)
                v1 = w1[0].rearrange("p (g c) -> p g c", c=16)
                v2 = w2[0].rearrange("p (g c) -> p g c", c=16)
                rd = [Tt[0][1], Tt[1][1], cpl[0][1], cpl[1][1]]
                p.op("dve", lambda e: e.tensor_tensor(out=v1, in0=c0, in1=tr, op=ALU.mult), reads=rd, writes=[w1[1]])
                p.op("dve", lambda e: e.tensor_tensor(out=v2, in0=c1, in1=ti, op=ALU.mult), reads=rd, writes=[w2[1]])
                tt_("dve", wri[0], w1, w2, ALU.add)
                p.op("dve", lambda e: e.tensor_tensor(out=v1, in0=c1, in1=tr, op=ALU.mult), reads=rd, writes=[w1[1]])
                p.op("dve", lambda e: e.tensor_tensor(out=v2, in0=c0, in1=ti, op=ALU.mult), reads=rd, writes=[w2[1]])
                tt_("dve", wri[1], w1, w2, ALU.subtract)

            nkt = 0
            for j in range(8):
                uv = uT[:, j, :].rearrange("p (n t) -> p t n", t=16)
                yv = hn[:, j, :].rearrange("p (n t) -> p t n", t=16)
                for ri in range(2):
                    p.op("act", lambda e, ri=ri, j=j: e.activation(
                        out=bplj[ri][0].rearrange("p (g c) -> p g c", c=16), in_=btmp[ri][0][:, j * 8:(j + 1) * 8, :], func=AF.Copy),
                        reads=[btmp[ri][1]], writes=[bplj[ri][1]])
                ybanks = [p.nb() for _ in range(4)]
                for b0 in ybanks:
                    p.reserved.add(b0)
                p.op("dve", lambda e, j=j: e.tensor_scalar(out=ddiag, in0=ident[:], scalar1=dsk[:, j:j + 1], scalar2=None, op0=ALU.mult),
                     reads=["ident", dskk], writes=[ddk])
                for bq in range(4):
                    p.op("pe", lambda e, bq=bq, yb=ybanks[bq], uv=uv: e.matmul(
                        ps[:, yb, :], ddiag, uv[:, 4 * bq:4 * bq + 4, :], start=True, stop=False),
                        reads=[ddk, utk], writes=[("ps", ybanks[bq])])
                for i in range(16):
                    make_w(TA, i, j)
                    kbs = (p.nb(), p.nb())
                    for dirn, P_, lag in ((0, PF, 15 - i), (1, PB, i)):
                        for ri in range(2):
                            p.op("pe", lambda e, kb=kbs[dirn], P_=P_, ri=ri: e.matmul(
                                ps[:, kb, 0:128], bplj[ri][0][P_, :], wri[ri][0][P_, :],
                                start=(ri == 0), stop=(ri == 1)),
                                reads=[bplj[ri][1], wri[ri][1]], writes=[("ps", kbs[dirn])])
                    for dirn, lag in ((0, 15 - i), (1, i)):
                        kt, ktk = ktile[nkt % 4]
                        nkt += 1
                        p.op("dve", lambda e, kt=kt, kb=kbs[dirn]: e.tensor_tensor(
                            out=kt, in0=ps[:, kb, 0:128], in1=blockmask, op=ALU.mult),
                            reads=[("ps", kbs[dirn]), bmk], writes=[ktk])
                        for bq in range(4):
                            if dirn == 0:
                                t0_, t1_ = max(lag, 4 * bq), 4 * bq + 4
                                src0 = t0_ - lag
                            else:
                                t0_, t1_ = 4 * bq, min(4 * bq + 4, 16 - lag)
                                src0 = t0_ + lag
                            if t1_ <= t0_:
                                continue
                            nt_ = t1_ - t0_
                            p.op("pe", lambda e, kt=kt, yb=ybanks[bq], t0_=t0_, nt_=nt_, src0=src0, bq=bq, uv=uv: e.matmul(
                                ps[:, yb, (t0_ - 4 * bq) * 128:(t0_ - 4 * bq + nt_) * 128], kt, uv[:, src0:src0 + nt_, :],
                                start=False, stop=False),
                                reads=[ktk, utk], writes=[("ps", ybanks[bq])])
                for g8 in range(8):
                    g = j * 8 + g8
                    p.op("dve", lambda e, j=j, g8=g8: e.tensor_scalar(out=ug_ap, in0=uT[:, j, :], scalar1=rowmask[:, g8:g8 + 1], scalar2=None, op0=ALU.mult),
                         reads=[utk, rmk_], writes=[ugk])
                    v0r, v0i = vpp[0][0], vpp[0][1]
                    for hs in range(2):
                        for ri in range(2):
                            for t2_ in range(2):
                                b0 = p.nb()
                                tok0 = hs * 1024 + t2_ * 512
                                p.op("pe", lambda e, b0=b0, ri=ri, j=j, tok0=tok0: e.matmul(
                                    ps[:, b0, :], bcl[ri][0][:, j, :], ug_ap[:, tok0:tok0 + 512], start=True, stop=True),
                                    reads=[bcl[ri][1], ugk], writes=[("ps", b0)])
                                p.op("act", lambda e, b0=b0, ri=ri, t2_=t2_: e.activation(
                                    out=zb[ri][0][:, t2_ * 512:(t2_ + 1) * 512], in_=ps[:, b0, :], func=AF.Copy),
                                    reads=[("ps", b0)], writes=[zb[ri][1]])
                        ns = slice(hs * 64, hs * 64 + 64)
                        nsp = slice(64 + hs * 64, 64 + hs * 64 + 64)
                        for sg in range(16):
                            zr = zb[0][0][:, sg:1024:16]
                            zi = zb[1][0][:, sg:1024:16]
                            ar_ = TA[0][0][:, sg, g:g + 1]
                            ai_ = TA[1][0][:, sg, g:g + 1]
                            rd = [zb[0][1], zb[1][1], TA[0][1], TA[1][1]]
                            if sg == 0:
                                p.op("dve", lambda e, zr=zr, ar_=ar_, ns=ns: e.tensor_scalar(out=accA[0][:, ns], in0=zr, scalar1=ar_, scalar2=None, op0=ALU.mult), reads=rd, writes=[accA[1]])
                                p.op("dve", lambda e, zi=zi, ai_=ai_, ns=ns: e.tensor_scalar(out=accB[0][:, ns], in0=zi, scalar1=ai_, scalar2=None, op0=ALU.mult), reads=rd, writes=[accB[1]])
                                p.op("dve", lambda e, zr=zr, ai_=ai_, nsp=nsp: e.tensor_scalar(out=v0i[0][:, nsp], in0=zr, scalar1=ai_, scalar2=None, op0=ALU.mult), reads=rd, writes=[v0i[1]])
                            else:
                                p.op("dve", lambda e, zr=zr, ar_=ar_, ns=ns: e.scalar_tensor_tensor(out=accA[0][:, ns], in0=zr, scalar=ar_, in1=accA[0][:, ns], op0=ALU.mult, op1=ALU.add), reads=rd + [accA[1]], writes=[accA[1]])
                                p.op("dve", lambda e, zi=zi, ai_=ai_, ns=ns: e.scalar_tensor_tensor(out=accB[0][:, ns], in0=zi, scalar=ai_, in1=accB[0][:, ns], op0=ALU.mult, op1=ALU.add), reads=rd + [accB[1]], writes=[accB[1]])
                                p.op("dve", lambda e, zr=zr, ai_=ai_, nsp=nsp: e.scalar_tensor_tensor(out=v0i[0][:, nsp], in0=zr, scalar=ai_, in1=v0i[0][:, nsp], op0=ALU.mult, op1=ALU.add), reads=rd + [v0i[1]], writes=[v0i[1]])
                            p.op("dve", lambda e, zi=zi, ar_=ar_, nsp=nsp: e.scalar_tensor_tensor(out=v0i[0][:, nsp], in0=zi, scalar=ar_, in1=v0i[0][:, nsp], op0=ALU.mult, op1=ALU.add), reads=rd + [v0i[1]], writes=[v0i[1]])
                    p.op("dve", lambda e: e.tensor_tensor(out=v0r[0][:, 64:192], in0=accA[0], in1=accB[0], op=ALU.subtract),
                         reads=[accA[1], accB[1]], writes=[v0r[1]])
                    cur = 0
                    for m in range(7):
                        s_ = 1 << m
                        last = (m == 6)
                        src = vpp[cur]
                        dstp = vpp[1 - cur]
                        ar_ = pw[0][0][:, m + 4, g:g + 1]
                        ai_ = pw[1][0][:, m + 4, g:g + 1]
                        nai_ = pw[2][0][:, m + 4, g:g + 1]
                        for d_ in range(2):
                            P_ = PF if d_ == 0 else PB
                            sh = -s_ if d_ == 0 else s_
                            if not last:
                                o_ = slice(64, 192); i_ = slice(64 + sh, 192 + sh); c_ = slice(64, 192)
                                dr_ = dstp[0][0][P_, o_]; di_ = dstp[1][0][P_, o_]
                                wkeys = [dstp[0][1], dstp[1][1]]
                            else:
                                if d_ == 0:
                                    c_ = slice(64, 191); i_ = slice(64 + sh, 191 + sh); oo = slice(1, 128)
                                else:
                                    c_ = slice(65, 192); i_ = slice(65 + sh, 192 + sh); oo = slice(0, 127)
                                dr_ = sx[P_, g8, 0, oo]; di_ = sx[P_, g8, 1, oo]
                                wkeys = [sxk, sxk]
                            rd = [src[0][1], src[1][1], pw[0][1], pw[1][1], pw[2][1]]
                            p.op("dve", lambda e, P_=P_, i_=i_, c_=c_, src=src, dr_=dr_, ar_=ar_: e.scalar_tensor_tensor(
                                out=dr_, in0=src[0][0][P_, i_], scalar=ar_[P_], in1=src[0][0][P_, c_], op0=ALU.mult, op1=ALU.add),
                                reads=rd, writes=[wkeys[0]])
                            p.op("dve", lambda e, P_=P_, i_=i_, src=src, dr_=dr_, nai_=nai_: e.scalar_tensor_tensor(
                                out=dr_, in0=src[1][0][P_, i_], scalar=nai_[P_], in1=dr_, op0=ALU.mult, op1=ALU.add),
                                reads=rd + [wkeys[0]], writes=[wkeys[0]])
                            p.op("dve", lambda e, P_=P_, i_=i_, c_=c_, src=src, di_=di_, ai_=ai_: e.scalar_tensor_tensor(
                                out=di_, in0=src[0][0][P_, i_], scalar=ai_[P_], in1=src[1][0][P_, c_], op0=ALU.mult, op1=ALU.add),
                                reads=rd, writes=[wkeys[1]])
                            p.op("dve", lambda e, P_=P_, i_=i_, src=src, di_=di_, ar_=ar_: e.scalar_tensor_tensor(
                                out=di_, in0=src[1][0][P_, i_], scalar=ar_[P_], in1=di_, op0=ALU.mult, op1=ALU.add),
                                reads=rd + [wkeys[1]], writes=[wkeys[1]])
                        cur = 1 - cur
                for tq in range(16):
                    make_w(TB, tq, j)
                    for ri in range(2):
                        wp, wpk = wpad[ri]
                        p.op("dve", lambda e, wp=wp, ri=ri: e.tensor_tensor(out=wp, in0=bcast_mid(wri[ri][0], 8), in1=colmask, op=ALU.mult),
                             reads=[wri[ri][1], cmk], writes=[wpk])
                    yb = ybanks[tq // 4]
                    for g8 in range(8):
                        for ri in range(2):
                            lastmm = (tq % 4 == 3 and g8 == 7 and ri == 1)
                            p.op("pe", lambda e, yb=yb, tq=tq, g8=g8, ri=ri, lastmm=lastmm: e.matmul(
                                ps[:, yb, (tq % 4) * 128:(tq % 4 + 1) * 128], wpad[ri][0][:, g8, :], sx[:, g8, ri, :],
                                start=False, stop=lastmm),
                                reads=[wpad[ri][1], sxk], writes=[("ps", yb)])
                for bq in range(4):
                    yb = ybanks[bq]
                    g1, g2 = gx[1], gx[2]
                    p.op("act", lambda e, yb=yb, g1=g1: e.activation(out=g1[0], in_=ps[:, yb, :], func=AF.Copy), reads=[("ps", yb)], writes=[g1[1]])
                    tt_("dve", g2, g1, g1, ALU.mult)
                    ts_("dve", g2, g2, 0.044715, 1.0, ALU.mult, ALU.add)
                    tt_("dve", g2, g2, g1, ALU.mult)
                    act_(g2, g2, AF.Sigmoid, 1.5957691216057308)
                    p.op("dve", lambda e, bq=bq, yv=yv, g1=g1, g2=g2: e.tensor_tensor(
                        out=yv[:, 4 * bq:4 * bq + 4, :], in0=g1[0].rearrange("p (t n) -> p t n", t=4),
                        in1=g2[0].rearrange("p (t n) -> p t n", t=4), op=ALU.mult),
                        reads=[g1[1], g2[1]], writes=[("hn", 0), ("hn", 1), ("hn", 2), ("hn", 3)])
                for b0 in ybanks:
                    p.reserved.discard(b0)
            A.off = mark
            p.barrier()
            wgl = A.alloc((128, 8, 2, 128), BF16)
            gx = [None, A.alloc((128, 512), F32), A.alloc((128, 512), F32)]
            wg, wgk = wgl
            wgl_v = w_glu_d.rearrange("(kc p) n -> p kc n", p=128)
            for oc in range(8):
                for gu in range(2):
                    p.op("pq", lambda e, oc=oc, gu=gu: e.dma_start(out=wg[:, :, gu, :], in_=wgl_v[:, :, gu * D + oc * 128: gu * D + (oc + 1) * 128]), writes=[wgk])
                for tt in range(4):
                    sl = slice(tt * 512, (tt + 1) * 512)
                    ba = p.nb(); bg = p.nb()
                    for gu, bb in ((0, ba), (1, bg)):
                        for kc in range(8):
                            p.op("pe", lambda e, bb=bb, gu=gu, kc=kc, sl=sl: e.matmul(
                                ps[:, bb, :], wg[:, kc, gu, :], hn[:, kc, sl], start=(kc == 0), stop=(kc == 7)),
                                reads=[wgk, ("hn", tt)], writes=[("ps", bb)])
                    p.op("act", lambda e, bg=bg, q1=gx[1][0]: e.activation(out=q1, in_=ps[:, bg, :], func=AF.Sigmoid), reads=[("ps", bg)], writes=[gx[1][1]])
                    p.op("dve", lambda e, ba=ba, q1=gx[1][0], q2=gx[2][0]: e.tensor_tensor(out=q2, in0=q1, in1=ps[:, ba, :], op=ALU.mult),
                         reads=[gx[1][1], ("ps", ba)], writes=[gx[2][1]])
                    p.op("dve", lambda e, oc=oc, sl=sl, q2=gx[2][0]: e.tensor_tensor(out=h[:, oc, sl], in0=h[:, oc, sl], in1=q2, op=ALU.add),
                         reads=[gx[2][1], ("h", tt)], writes=[("h", tt)])

        outs = []
        for s in range(nseq):
            load_x(s)
            if "attn" in stages:
                attn()
            if "ffn0" in stages:
                ffn(0, 1)
            if "s5" in stages:
                s5()
            if "ffn1" in stages:
                ffn(1, 3)
            if "final" in stages:
                rms_norm(4, h, "h", in_place=True)
            outs += store_y(s, h)
        p.emit(final_wait_ops=outs)
    return nc


def make_in_map(P, xs):
    gains = np.stack([P["attn_norm"][0], P["ffn_norm"][0], P["ssm_norm"][0], P["ffn_norm"][1],
                      P["final_norm"]]).astype(np.float32)
    return {
        "x": np.ascontiguousarray(xs.reshape(-1, D)),
        "gains": gains,
        "w_qkv": np.ascontiguousarray(P["w_qkv"][0]),
        "w_attn_out": np.ascontiguousarray(P["w_attn_out"][0]),
        "w_ffn_in0": np.ascontiguousarray(P["w_ffn_in"][0]),
        "w_ffn_in1": np.ascontiguousarray(P["w_ffn_in"][1]),
        "w_ffn_out0": np.ascontiguousarray(P["w_ffn_out"][0]),
        "w_ffn_out1": np.ascontiguousarray(P["w_ffn_out"][1]),
        "w_ssm_in": np.ascontiguousarray(P["w_ssm_in"][0]),
        "a_re": np.ascontiguousarray(P["a_re"][0]),
        "a_im": np.ascontiguousarray(P["a_im"][0]),
        "log_dt": np.ascontiguousarray(P["log_dt"][0]),
        "b_re": np.ascontiguousarray(P["b_re"][0]),
        "b_im": np.ascontiguousarray(P["b_im"][0]),
        "c_re": np.ascontiguousarray(P["c_re"][0]),
        "c_im": np.ascontiguousarray(P["c_im"][0]),
        "d_skip": np.ascontiguousarray(P["d_skip"][0]),
        "w_glu": np.ascontiguousarray(P["w_glu"][0]),
    }


def kernel(**inputs):
    P = {k: np.asarray(v) for k, v in inputs.items()}
    x = P["x"]
    nc = build(nseq=2)
    in_maps = [make_in_map(P, x[2 * c:2 * c + 2]) for c in range(8)]
    res = run_bass_kernel_spmd(nc, in_maps, core_ids=list(range(8)))
    out = np.concatenate([np.asarray(r["y"]).reshape(2, S, D) for r in res.results], axis=0)
    return out.astype(np.float32)
```
